# Optimizing a Trainium2 kernel written in Bass

```python
import math
import functools
import jax
import jax.numpy as jnp
from jax import lax
import numpy as np

D_MODEL = 1024
BATCH = 4
SEQ = 4096
DEPTH = 4
DEC_BATCH = 128
DEC_SEQ = 8
PAST_LEN = 2048
PAGE_SIZE = 128

ATT_WIDTH = D_MODEL // 2
POOL_WIDTH = D_MODEL - ATT_WIDTH
HEAD_DIM = 64
N_HEADS = ATT_WIDTH // HEAD_DIM
N_KV = 2
GROUP = N_HEADS // N_KV
CMP_LEN = 32
CMP_STRIDE = 16
SEL_BLOCK = 64
N_SELECT = 16
WINDOW = 512
N_BRANCH = 3
POOL_WINDOWS = (2, 4, 8, 16)
N_POOL_GROUPS = len(POOL_WINDOWS)
POOL_GROUP_DIM = POOL_WIDTH // N_POOL_GROUPS
POOL_STATE = max(POOL_WINDOWS) - 1
N_BUCKETS = 32
MAX_DISTANCE = 128
D_FF = -(-8 * D_MODEL // (3 * 256)) * 256
Q_BLOCK = 128
EPS = 1e-6
NEG_INF = -1e30
Q_COLS = N_HEADS * HEAD_DIM
KV_COLS = 2 * N_KV * HEAD_DIM
GATE_COLS = N_HEADS * N_BRANCH
IN_COLS = Q_COLS + N_BRANCH * KV_COLS + GATE_COLS + POOL_WIDTH

kernel_name = "hymba_nsa_pool_decoder_step"


def rms_norm(x, g):
    xf = x.astype(jnp.float32)
    y = xf * lax.rsqrt(jnp.mean(xf * xf, axis=-1, keepdims=True) + EPS)
    return (y * g.astype(jnp.float32)).astype(x.dtype)


def rel_bucket(dist):
    n = jnp.maximum(dist, 0)
    max_exact = N_BUCKETS // 2
    nf = jnp.maximum(n, 1).astype(jnp.float32)
    large = max_exact + (jnp.log(nf / max_exact) / math.log(MAX_DISTANCE / max_exact)
                         * (N_BUCKETS - max_exact)).astype(jnp.int32)
    return jnp.where(n < max_exact, n, jnp.minimum(large, N_BUCKETS - 1))


def masked_softmax(s, mask):
    s = jnp.where(mask, s.astype(jnp.float32), NEG_INF)
    p = jax.nn.softmax(s, axis=-1)
    return jnp.where(mask, p, 0.0)


def compress_blocks(kv, cmp_pos, w_cmp):
    B, Tk = kv.shape[:2]
    ratio = CMP_LEN // CMP_STRIDE
    n_sub = Tk // CMP_STRIDE
    n_cmp = n_sub - ratio + 1
    sub = kv.reshape(B, n_sub, CMP_STRIDE, 2, N_KV, HEAD_DIM)
    pe = cmp_pos.reshape(ratio, CMP_STRIDE, 2, HEAD_DIM)
    w = w_cmp.reshape(2, ratio, CMP_STRIDE, HEAD_DIM, HEAD_DIM)
    out = 0
    for r in range(ratio):
        part = jnp.einsum('bsljgd,jlde->bsjge', sub + pe[r][None, None, :, :, None, :], w[:, r])
        out = out + part[:, r:r + n_cmp]
    return out


def block_importance(p_cmp, n_sel):
    ratio = CMP_LEN // CMP_STRIDE
    spb = SEL_BLOCK // CMP_STRIDE
    pad = [(0, 0)] * (p_cmp.ndim - 1) + [(ratio - 1, ratio - 1)]
    P = jnp.pad(p_cmp, pad)
    return sum(P[..., o:o + spb * n_sel:spb] for o in range(spb + ratio - 1))


def gather_blocks(blocks, idx):
    return jax.vmap(jax.vmap(lambda bl, ix: bl[ix]))(blocks, idx)


def nsa_block(q, gates, q_pos, comp, sel_blocks, wkv, w_pos, rel_bias):
    B, Tq = q.shape[:2]
    qg = q.reshape(B, Tq, N_KV, GROUP, HEAD_DIM)
    bias_hd = rel_bias.astype(jnp.float32).reshape(N_BUCKETS, N_KV, GROUP)

    n_cmp = comp.shape[1]
    c_end = CMP_STRIDE * jnp.arange(n_cmp) + CMP_LEN - 1
    dist = q_pos[:, None] - c_end[None, :]
    s = jnp.einsum('bqgrd,bcgd->bgrqc', qg, comp[:, :, 0]).astype(jnp.float32)
    s = s + bias_hd[rel_bucket(dist)].transpose(2, 3, 0, 1)
    p_cmp = masked_softmax(s, dist >= 0)
    o_cmp = jnp.einsum('bgrqc,bcgd->bqgrd', p_cmp.astype(comp.dtype), comp[:, :, 1])

    n_sel = sel_blocks.shape[2]
    imp = block_importance(p_cmp.sum(axis=2), n_sel)
    cur = q_pos // SEL_BLOCK
    j = jnp.arange(n_sel)[None, :]
    forced = (j == 0) | (j == cur[:, None]) | (j == cur[:, None] - 1)
    score = jnp.where(forced, jnp.inf, imp)
    score = jnp.where(j > cur[:, None], -jnp.inf, score)
    _, idx = lax.top_k(score, min(N_SELECT, n_sel))
    kv_g = gather_blocks(sel_blocks, idx)
    k_pos = idx[..., None] * SEL_BLOCK + jnp.arange(SEL_BLOCK)
    dist = q_pos[:, None, None] - k_pos
    s = jnp.einsum('bqgrd,bgqksd->bgrqks', qg, kv_g[..., 0, :]).astype(jnp.float32)
    g_idx = jnp.arange(N_KV)[None, :, None, None, None]
    bias = bias_hd.transpose(1, 0, 2)[g_idx, rel_bucket(dist)]
    s = s + bias.transpose(0, 1, 5, 2, 3, 4)
    shp = s.shape
    mask = (dist >= 0).reshape(B, N_KV, 1, Tq, -1)
    p = masked_softmax(s.reshape(*shp[:4], -1), mask).reshape(shp)
    o_sel = jnp.einsum('bgrqks,bgqksd->bqgrd', p.astype(kv_g.dtype), kv_g[..., 1, :])

    dist = q_pos[:, None] - w_pos[None, :]
    mask = (dist >= 0) & (dist < WINDOW) & (w_pos[None, :] >= 0)
    s = jnp.einsum('bqgrd,bwgd->bgrqw', qg, wkv[:, :, 0]).astype(jnp.float32)
    s = s + bias_hd[rel_bucket(dist)].transpose(2, 3, 0, 1)
    p = masked_softmax(s, mask)
    o_win = jnp.einsum('bgrqw,bwgd->bqgrd', p.astype(wkv.dtype), wkv[:, :, 1])

    g = gates.reshape(B, Tq, N_KV, GROUP, N_BRANCH)
    o = g[..., 0:1] * o_cmp + g[..., 1:2] * o_sel + g[..., 2:3] * o_win
    return o.reshape(B, Tq, Q_COLS)


def nsa_prompt(q, gates, kv_cmp, kv_sel, kv_win, cmp_pos, w_cmp, rel_bias):
    B, T = q.shape[:2]
    comp = compress_blocks(kv_cmp, cmp_pos, w_cmp)
    sel_blocks = kv_sel.reshape(B, T // SEL_BLOCK, SEL_BLOCK, 2, N_KV, HEAD_DIM).transpose(0, 4, 1, 2, 3, 5)
    win_pad = jnp.pad(kv_win, ((0, 0), (WINDOW, 0), (0, 0), (0, 0), (0, 0)))

    def one_block(s0):
        q_pos = s0 + jnp.arange(Q_BLOCK)
        qb = lax.dynamic_slice_in_dim(q, s0, Q_BLOCK, axis=1)
        gb = lax.dynamic_slice_in_dim(gates, s0, Q_BLOCK, axis=1)
        wkv = lax.dynamic_slice_in_dim(win_pad, s0, WINDOW + Q_BLOCK, axis=1)
        w_pos = s0 - WINDOW + jnp.arange(WINDOW + Q_BLOCK)
        return nsa_block(qb, gb, q_pos, comp, sel_blocks, wkv, w_pos, rel_bias)

    out = lax.map(one_block, jnp.arange(T // Q_BLOCK) * Q_BLOCK)
    return out.transpose(1, 0, 2, 3).reshape(B, T, Q_COLS)


def nsa_sample(q, gates, kv_cmp, kv_sel, kv_win, cache_cmp, cache_sel, win_buf, page_table,
               cmp_pos, w_cmp, rel_bias):
    B, Tq = q.shape[:2]
    past = page_table.shape[1] * PAGE_SIZE

    def with_past(cache, new):
        rows = cache[page_table].reshape(B, past, 2, N_KV, HEAD_DIM)
        full = jnp.concatenate([rows, new], axis=1)
        t_pad = -(-full.shape[1] // SEL_BLOCK) * SEL_BLOCK
        return jnp.pad(full, ((0, 0), (0, t_pad - full.shape[1]), (0, 0), (0, 0), (0, 0)))

    full_cmp = with_past(cache_cmp, kv_cmp)
    full_sel = with_past(cache_sel, kv_sel)
    comp = compress_blocks(full_cmp, cmp_pos, w_cmp)
    n_sel = full_sel.shape[1] // SEL_BLOCK
    sel_blocks = full_sel.reshape(B, n_sel, SEL_BLOCK, 2, N_KV, HEAD_DIM).transpose(0, 4, 1, 2, 3, 5)
    wb = win_buf.shape[1]
    wkv = jnp.concatenate([win_buf, kv_win], axis=1)
    w_pos = past - wb + jnp.arange(wb + Tq)
    q_pos = past + jnp.arange(Tq)
    return nsa_block(q, gates, q_pos, comp, sel_blocks, wkv, w_pos, rel_bias)


def pool_mix(u_prev, u, pos0, w_pool, pool_scale):
    B, T, C = u.shape
    ext = jnp.concatenate([u_prev, u], axis=1)
    cs = jnp.pad(jnp.cumsum(ext.astype(jnp.float32), axis=1), ((0, 0), (1, 0), (0, 0)))
    pos = pos0 + jnp.arange(T)
    end = cs[:, POOL_STATE + 1:]
    means = []
    for k, w in enumerate(POOL_WINDOWS):
        ch = slice(k * POOL_GROUP_DIM, (k + 1) * POOL_GROUP_DIM)
        win_sum = end[..., ch] - cs[:, POOL_STATE + 1 - w:POOL_STATE + 1 - w + T, ch]
        cnt = jnp.minimum(pos + 1, w).astype(jnp.float32)[None, :, None]
        means.append(win_sum / cnt)
    pooled = (jnp.concatenate(means, axis=-1) - u.astype(jnp.float32)).astype(u.dtype)
    pooled = pooled.reshape(B, T, N_POOL_GROUPS, POOL_GROUP_DIM)
    out = jnp.einsum('btkc,kcd->btkd', pooled, w_pool).reshape(B, T, C) * pool_scale
    return out, ext[:, -POOL_STATE:]


def trunk_layer(x, nsa_fn, pool_prev, pos0, norm_mix, norm_ffn, w_in, w_out, w_pool, pool_scale,
                w_ffn_in, w_ffn_out):
    B, T, _ = x.shape
    h = rms_norm(x, norm_mix)
    z = h @ w_in
    q = z[..., :Q_COLS].reshape(B, T, N_HEADS, HEAD_DIM) * HEAD_DIM ** -0.5
    kvs = [z[..., Q_COLS + i * KV_COLS:Q_COLS + (i + 1) * KV_COLS].reshape(B, T, 2, N_KV, HEAD_DIM)
           for i in range(N_BRANCH)]
    off = Q_COLS + N_BRANCH * KV_COLS
    gates = jax.nn.sigmoid(z[..., off:off + GATE_COLS]).reshape(B, T, N_HEADS, N_BRANCH)
    u = z[..., off + GATE_COLS:]
    o_att = nsa_fn(q, gates, kvs[0], kvs[1], kvs[2])
    o_pool, pool_state = pool_mix(pool_prev, u, pos0, w_pool, pool_scale)
    x = x + jnp.concatenate([o_att, o_pool], axis=-1) @ w_out
    h = rms_norm(x, norm_ffn)
    gate, up = jnp.split(h @ w_ffn_in, 2, axis=-1)
    x = x + (jax.nn.silu(gate) * up) @ w_ffn_out
    return x, kvs, pool_state


def setup_inputs(seed: int = 0) -> dict:
    key = jax.random.key(seed)
    ks = jax.random.split(key, 20)
    n_pages = PAST_LEN // PAGE_SIZE
    n_used = DEC_BATCH * n_pages
    n_phys = n_used + -(-n_used // 4)
    wb = min(WINDOW, PAST_LEN)

    def nrm(k, shape, scale):
        return scale * jax.random.normal(k, shape, jnp.float32)

    page_table = jax.random.permutation(ks[0], n_phys)[:n_used].reshape(DEC_BATCH, n_pages).astype(jnp.int32)
    return {
        'x_prompt': nrm(ks[1], (BATCH, SEQ, D_MODEL), 1.0),
        'x_sample': nrm(ks[2], (DEC_BATCH, DEC_SEQ, D_MODEL), 1.0),
        'cache_cmp': nrm(ks[3], (DEPTH, n_phys, PAGE_SIZE, 2, N_KV, HEAD_DIM), 1.0),
        'cache_sel': nrm(ks[4], (DEPTH, n_phys, PAGE_SIZE, 2, N_KV, HEAD_DIM), 1.0),
        'state_win': nrm(ks[5], (DEPTH, DEC_BATCH, wb, 2, N_KV, HEAD_DIM), 1.0),
        'state_pool': nrm(ks[6], (DEPTH, DEC_BATCH, POOL_STATE, POOL_WIDTH), 1.0),
        'page_table': page_table,
        'rel_bias': nrm(ks[7], (N_BUCKETS, N_HEADS), 0.5),
        'norm_mix': 1.0 + nrm(ks[8], (DEPTH, D_MODEL), 0.05),
        'norm_ffn': 1.0 + nrm(ks[9], (DEPTH, D_MODEL), 0.05),
        'norm_final': 1.0 + nrm(ks[10], (D_MODEL,), 0.05),
        'w_in': nrm(ks[11], (DEPTH, D_MODEL, IN_COLS), D_MODEL ** -0.5),
        'w_out': nrm(ks[12], (DEPTH, ATT_WIDTH + POOL_WIDTH, D_MODEL), D_MODEL ** -0.5),
        'cmp_pos': nrm(ks[13], (DEPTH, CMP_LEN, 2, HEAD_DIM), 0.1),
        'w_cmp': nrm(ks[14], (DEPTH, 2, CMP_LEN, HEAD_DIM, HEAD_DIM), (CMP_LEN * HEAD_DIM) ** -0.5),
        'w_pool': nrm(ks[15], (DEPTH, N_POOL_GROUPS, POOL_GROUP_DIM, POOL_GROUP_DIM), POOL_GROUP_DIM ** -0.5),
        'pool_scale': 1.0 + nrm(ks[16], (DEPTH, POOL_WIDTH), 0.05),
        'w_ffn_in': nrm(ks[17], (DEPTH, D_MODEL, 2 * D_FF), D_MODEL ** -0.5),
        'w_ffn_out': nrm(ks[18], (DEPTH, D_FF, D_MODEL), D_FF ** -0.5),
    }


def reference(x_prompt, x_sample, cache_cmp, cache_sel, state_win, state_pool, page_table, rel_bias,
              norm_mix, norm_ffn, norm_final, w_in, w_out, cmp_pos, w_cmp, w_pool, pool_scale,
              w_ffn_in, w_ffn_out):
    past = page_table.shape[1] * PAGE_SIZE
    wb = state_win.shape[2]
    xp, xs = x_prompt, x_sample
    pool_zero = jnp.zeros((x_prompt.shape[0], POOL_STATE, POOL_WIDTH), x_prompt.dtype)
    p_cmp, p_sel, p_win, p_pool = [], [], [], []
    s_cmp, s_sel, s_win, s_pool = [], [], [], []
    for l in range(DEPTH):
        lw = (norm_mix[l], norm_ffn[l], w_in[l], w_out[l], w_pool[l], pool_scale[l], w_ffn_in[l], w_ffn_out[l])
        prompt_nsa = functools.partial(nsa_prompt, cmp_pos=cmp_pos[l], w_cmp=w_cmp[l], rel_bias=rel_bias)
        xp, kvp, pool_p = trunk_layer(xp, prompt_nsa, pool_zero, 0, *lw)
        p_cmp.append(kvp[0])
        p_sel.append(kvp[1])
        p_win.append(jnp.pad(kvp[2], ((0, 0), (wb, 0), (0, 0), (0, 0), (0, 0)))[:, -wb:])
        p_pool.append(pool_p)
        sample_nsa = functools.partial(nsa_sample, cache_cmp=cache_cmp[l], cache_sel=cache_sel[l],
                                       win_buf=state_win[l], page_table=page_table,
                                       cmp_pos=cmp_pos[l], w_cmp=w_cmp[l], rel_bias=rel_bias)
        xs, kvs, pool_s = trunk_layer(xs, sample_nsa, state_pool[l], past, *lw)
        s_cmp.append(kvs[0])
        s_sel.append(kvs[1])
        s_win.append(jnp.concatenate([state_win[l], kvs[2]], axis=1)[:, -wb:])
        s_pool.append(pool_s)
    y_prompt = rms_norm(xp, norm_final)
    y_sample = rms_norm(xs, norm_final)
    return (y_prompt, y_sample,
            jnp.stack(p_cmp), jnp.stack(p_sel), jnp.stack(p_win), jnp.stack(p_pool),
            jnp.stack(s_cmp), jnp.stack(s_sel), jnp.stack(s_win), jnp.stack(s_pool))
```

```python
import contextlib
import numpy as np
import ml_dtypes
import concourse.bass as bass
import concourse.mybir as mybir
from concourse.bass_utils import run_bass_kernel_spmd

F32 = mybir.dt.float32
BF16 = mybir.dt.bfloat16
I32 = mybir.dt.int32
AF = mybir.ActivationFunctionType
ALU = mybir.AluOpType

D = 1024
INC = 1816
DFF = 2816
NEG = -30000.0
EPS = 1e-6
NSEQ = 1152
NCORES = 8
DO_SAMPLE = True


class Buf:
    __slots__ = ("name", "last_w", "readers", "excl")

    def __init__(self, name, excl=False):
        self.name = name
        self.excl = excl
        self.last_w = None
        self.readers = {}


class _Rec:
    def __getattr__(self, name):
        def f(*a, **k):
            self.call = (name, a, k)
            return self
        return f


class KB:
    ENG = ("sp", "act", "pe", "dve", "pool")
    NDMA = {"sp": 8, "act": 2, "pool": 6}

    def __init__(self, nc):
        self.nc = nc
        self.prog = {e: [] for e in self.ENG}
        self.cnt = {e: 0 for e in self.ENG}
        self.waited = {e: {} for e in self.ENG}
        self.pending = {e: {} for e in self.ENG}
        self.dma_i = {e: 0 for e in self.NDMA}
        self.dma_val = {}
        self.final_tokens = []

    def _deps(self, eng, r, w):
        deps = dict(self.pending[eng])
        self.pending[eng] = {}

        def add(tok):
            if tok is None:
                return
            k, v = tok
            if deps.get(k, 0) < v:
                deps[k] = v
        for b in r:
            add(b.last_w)
        for b in w:
            add(b.last_w)
            for k, v in b.readers.items():
                if k == eng:
                    continue
                add((k, v))
        out = []
        wd = self.waited[eng]
        for k, v in deps.items():
            if k == "pe" and eng == "pe":
                continue
            if wd.get(k, 0) >= v:
                continue
            wd[k] = v
            out.append((k, v))
        return out

    def op(self, eng, fn, r=(), w=(), dma=False, final=False):
        if not isinstance(fn, tuple):
            rec = _Rec()
            fn(rec)
            fn = rec.call
        ex = [b for b in r if b.excl]
        if ex:
            w = list(w) + [b for b in ex if b not in w]
            r = [b for b in r if not b.excl]
        waits = self._deps(eng, r, w)
        if dma:
            slot = self.dma_i[eng] % self.NDMA[eng]
            self.dma_i[eng] += 1
            key = ("dma", eng, slot)
            val = self.dma_val.get(key, 0) + 16
            self.dma_val[key] = val
            tok = (key, val)
            inc = (key, 16)
        else:
            self.cnt[eng] += 1
            tok = (eng, self.cnt[eng])
            inc = (eng, 1)
        self.prog[eng].append((waits, fn, inc))
        k, v = tok
        for b in r:
            if b.readers.get(k, 0) < v:
                b.readers[k] = v
        for b in w:
            b.last_w = tok
            b.readers = {}
        if final:
            self.final_tokens.append(tok)
        return tok

    def dma(self, eng, out, in_, r=(), w=(), final=False, **kw):
        return self.op(eng, ("dma_start", (), dict(out=out, in_=in_, **kw)), r=r, w=w, dma=True, final=final)

    def barrier(self):
        allv = {e: self.cnt[e] for e in self.ENG if self.cnt[e] > 0}
        allv.update(self.dma_val)
        for e in self.ENG:
            for k, v in allv.items():
                if k == e:
                    continue
                if self.pending[e].get(k, 0) < v:
                    self.pending[e][k] = v

    def emit(self):
        nc = self.nc
        keys = list(self.ENG) + [("dma", e, s) for e in self.NDMA for s in range(self.NDMA[e])]
        with contextlib.ExitStack() as st:
            sems = {}
            for k in keys:
                nm = k if isinstance(k, str) else "d_%s_%d" % (k[1], k[2])
                sems[k] = st.enter_context(nc.semaphore("s_" + nm))
            block = st.enter_context(nc.Block())
            fin = {}
            for k, v in self.final_tokens:
                if fin.get(k, 0) < v:
                    fin[k] = v

            def run(ename, e):
                for waits, fn, inc in self.prog[ename]:
                    for k, v in waits:
                        e.wait_ge(sems[k], v)
                    ins = getattr(e, fn[0])(*fn[1], **fn[2])
                    ins.then_inc(sems[inc[0]], inc[1])
                if ename == "sp":
                    for k, v in fin.items():
                        e.wait_ge(sems[k], v)

            @block.sync
            def _(e):
                run("sp", e)

            @block.scalar
            def _(e):
                run("act", e)

            @block.tensor
            def _(e):
                run("pe", e)

            @block.vector
            def _(e):
                run("dve", e)

            @block.gpsimd
            def _(e):
                run("pool", e)


def _bucket(n):
    n = np.maximum(n, 0)
    nf = np.maximum(n, 1).astype(np.float32)
    large = 16 + (np.log(nf / np.float32(16)) / np.float32(np.log(8.0)) * np.float32(16)).astype(np.int32)
    return np.where(n < 16, n, np.minimum(large, 31))


def host_consts(TE):
    c = {}
    c["ident"] = np.eye(128, dtype=np.float32).astype(ml_dtypes.bfloat16)
    c["j32"] = np.eye(128, dtype=np.float32)[::-1].copy()
    oh = np.zeros((33, NSEQ), np.float32)

    def put(col, n):
        if n < 0:
            oh[32, col] = 1.0
        else:
            oh[int(_bucket(np.array([n]))[0]), col] += 1.0
            oh[31, col] -= 1.0
    for n in range(384):
        put(n, n - 127)
    for n in range(256):
        if n >= 127:
            oh[32, 384 + n] = 1.0
    for s_i, s in enumerate((0, 128)):
        for m in range(256):
            put(640 + 256 * s_i + m, 127 + s - m)
    c["oh"] = oh
    c["negrow"] = np.full((1, 8), NEG, np.float32)
    ex = np.zeros((64, TE), np.float32)
    for j in range(64):
        ex[j, j * 64:(j + 1) * 64] = 1.0
    c["ex"] = ex.astype(ml_dtypes.bfloat16)
    fc = np.zeros((128, 128), np.float32)
    for q in range(128):
        cur = 1 if q >= 64 else 0
        for jj in range(128):
            jr = jj - 64
            if jr == cur or jr == cur - 1:
                fc[q, jj] = 1e9
            elif jr > cur:
                fc[q, jj] = -1e9
    c["fc"] = fc
    pb = np.zeros((6, 4, 128, 128), np.float32)
    for k, w in enumerate((2, 4, 8, 16)):
        for t in range(128):
            for tp in range(128):
                if 0 <= t - tp < w:
                    pb[0, k, tp, t] += 1.0 / w
                    pb[2, k, tp, t] += 1.0 / min(t + 1, w)
                    if tp // 8 == t // 8:
                        pb[3, k, tp, t] += 1.0 / w
                if t + 128 - tp < w:
                    pb[1, k, tp, t] += 1.0 / w
            pb[0, k, t, t] -= 1.0
            pb[2, k, t, t] -= 1.0
            pb[3, k, t, t] -= 1.0
        for hf in range(2):
            for rp in range(120):
                b = hf * 8 + rp // 15
                r = rp % 15
                for qi in range(8):
                    if (15 - r) + qi < w:
                        pb[4 + hf, k, rp, b * 8 + qi] = 1.0 / w
    c["pb"] = pb.astype(ml_dtypes.bfloat16)
    selb = np.zeros((9, 16, 128), np.float32)
    for b in range(16):
        for kp in range(8):
            selb[kp, b, b * 8 + kp] = 1.0
        for k in range(128):
            if k // 8 != b:
                selb[8, b, k] = 1.0
    c["selb"] = selb
    c["neg64"] = np.full((1, 64), NEG, np.float32)
    c["iota_p"] = np.arange(128, dtype=np.float32).reshape(128, 1)
    return c


def build(T, L, NSEL, NPHYS, do_sample=True, do_ffn=True, do_attn=True, stage=99):
    NT = T // 128
    TE = max(T, 2304)
    nc = bass.Bass("TRN2", target_bir_lowering=False)
    _LAST_NC[0] = nc

    def din(name, shape, dt=F32):
        return nc.dram_tensor(name, list(shape), dt, kind="ExternalInput")

    def dout(name, shape, dt=F32):
        return nc.dram_tensor(name, list(shape), dt, kind="ExternalOutput")

    def dscr(name, shape, dt=F32):
        return nc.dram_tensor(name, list(shape), dt, kind="Internal")

    xp = din("xp", [T, D])
    xs = din("xs", [128, D])
    ccmp = din("ccmp", [L, NPHYS * 128, 256])
    csel = din("csel", [L, NPHYS * 128, 256])
    swin = din("swin", [L, 16, 512, 256])
    spool = din("spool", [L, 16, 15, 512])
    ptab = din("ptab", [16, 16], I32)
    rel_bias = din("rel_bias", [32, 8])
    norm_mix = din("norm_mix", [L, D])
    norm_ffn = din("norm_ffn", [L, D])
    norm_final = din("norm_final", [1, D])
    w_in = din("w_in", [L, D, INC])
    w_out = din("w_out", [L, D, D])
    cmp_pos = din("cmp_pos", [L, 32, 2, 64])
    w_cmp = din("w_cmp", [L, 2, 32, 64, 64])
    w_pool = din("w_pool", [L, 4, 128, 128])
    pool_scale = din("pool_scale", [L, 512])
    w_ffn_in = din("w_ffn_in", [L, D, 2 * DFF])
    w_ffn_out = din("w_ffn_out", [L, DFF, D])
    c_ident = din("ident", [128, 128], BF16)
    c_j32 = din("j32", [128, 128])
    c_oh = din("oh", [33, NSEQ])
    c_negrow = din("negrow", [1, 8])
    c_ex = din("ex", [64, TE], BF16)
    c_fc = din("fc", [128, 128])
    c_pb = din("pb", [6, 4, 128, 128], BF16)
    c_selb = din("selb", [9, 16, 128])
    c_neg64 = din("neg64", [1, 64])
    c_iota = din("iota_p", [128, 1])

    y_p = dout("y_p", [T, D])
    y_s = dout("y_s", [128, D])
    ncmp_p = dout("ncmp_p", [L, T, 256])
    nsel_p = dout("nsel_p", [L, T, 256])
    nwin_p = dout("nwin_p", [L, 512, 256])
    npool_p = dout("npool_p", [L, 15, 512])
    ncmp_s = dout("ncmp_s", [L, 128, 256])
    nsel_s = dout("nsel_s", [L, 128, 256])
    nwin_s = dout("nwin_s", [L, 16, 512, 256])
    npool_s = dout("npool_s", [L, 16, 15, 512])

    xscr_p = dscr("xscr_p", [T, D])
    xscr_s = dscr("xscr_s", [128, D])
    fvd = dscr("fvd", [8, NSEQ])

    kb = KB(nc)
    B_xscr_p = [Buf("xscr_p%d" % i) for i in range(NT)]
    B_xscr_s = Buf("xscr_s")
    B_fvd = Buf("fvd")

    def bc_ap(ap, shape_tail):
        a = [list(x) for x in ap.ap]
        for s in shape_tail:
            a.append([0, s])
        return bass.AP(tensor=ap.tensor, offset=ap.offset, ap=a)

    with contextlib.ExitStack() as st:
        uid = [0]

        def un(name):
            uid[0] += 1
            return "t%d_%s" % (uid[0], name)

        def sb(name, shape, dt):
            return st.enter_context(nc.sbuf_tensor(un(name), list(shape), dt)), Buf(name)

        def psb(name, shape, dt):
            return st.enter_context(nc.psum_tensor(un(name), list(shape), dt)), Buf(name, excl=True)

        TB = [psb("tb%d" % i, [128, 1024], BF16) for i in range(2)]
        WB = [psb("wb%d" % i, [128, 512], F32) for i in range(4)]
        OB = [psb("ob%d" % i, [128, 512], F32) for i in range(2)]
        rot = {"tb": 0, "wb": 0}

        def next_tb():
            rot["tb"] += 1
            return TB[rot["tb"] % 2]

        def next_wb():
            rot["wb"] += 1
            return WB[rot["wb"] % 4]

        ident, B_ident = sb("ident", [128, 128], BF16)
        F0, B_F0 = sb("F0", [128, 8, 128], F32)
        F1, B_F1 = sb("F1", [128, 8, 128], F32)
        FW, B_FW = sb("FW", [128, 8, 128], F32)
        GN, B_GN = sb("GN", [128, 8, 16], F32)
        FC, B_FC = sb("FC", [128, 128], F32)
        PB, B_PB = sb("PB", [128, 24, 128], BF16)
        big9, B_big9 = sb("big9", [128, 1], F32)
        IDX, B_IDX = sb("IDX", [128, 256], I32)
        FSB, B_FSB = sb("FSB", [128, 16, 64], F32)

        kb.dma("sp", ident[:], c_ident.ap(), w=[B_ident])
        kb.dma("sp", FC[:], c_fc.ap(), w=[B_FC])
        kb.dma("sp", PB[:].rearrange("p (a b) t -> p a b t", a=6), c_pb.ap().rearrange("a b p t -> p a b t"), w=[B_PB])
        kb.op("pool", lambda e: e.memset(big9[:], 1e9), w=[B_big9])

        with contextlib.ExitStack() as st2:
            def sb2(name, shape, dt):
                return st2.enter_context(nc.sbuf_tensor(un(name), list(shape), dt)), Buf(name)
            tbl, B_tbl = sb2("tbl", [33, 8], F32)
            ohs, B_ohs = sb2("ohs", [33, NSEQ], F32)
            fvs, B_fvs = sb2("fvs", [8, NSEQ], F32)
            j32, B_j32 = sb2("j32", [128, 128], F32)
            hk, B_hk = sb2("hk", [128, 8, 128], F32)
            T0, B_T0 = sb2("T0", [128, 8, 128], F32)
            T1, B_T1 = sb2("T1", [128, 8, 128], F32)
            kb.dma("sp", tbl[0:32, :], rel_bias.ap(), w=[B_tbl])
            kb.dma("sp", tbl[32:33, :], c_negrow.ap(), w=[B_tbl])
            kb.dma("sp", ohs[:], c_oh.ap(), w=[B_ohs])
            kb.dma("sp", j32[:], c_j32.ap(), w=[B_j32])
            for ci, (c0, c1) in enumerate(((0, 512), (512, 1024), (1024, NSEQ))):
                wbt, wbb = next_wb()
                kb.op("pe", lambda e, wbt=wbt, c0=c0, c1=c1: e.matmul(wbt[0:8, 0:c1 - c0], lhsT=tbl[:, :], rhs=ohs[:, c0:c1], start=True, stop=True),
                      r=[B_tbl, B_ohs], w=[wbb])
                kb.op("dve", lambda e, wbt=wbt, c0=c0, c1=c1: e.tensor_copy(out=fvs[:, c0:c1], in_=wbt[0:8, 0:c1 - c0]), r=[wbb], w=[B_fvs])
            kb.dma("sp", fvd.ap(), fvs[:], r=[B_fvs], w=[B_fvd])
            for dst, Bdst, off in ((F0, B_F0, 0), (F1, B_F1, 128), (FW, B_FW, 384), (T0, B_T0, 640), (T1, B_T1, 896)):
                src = bass.AP(tensor=fvd, offset=off, ap=[[1, 128], [NSEQ, 8], [1, 128]])
                kb.dma("sp", hk[:], src, r=[B_fvd], w=[B_hk])
                for hh in range(2):
                    wbt, wbb = next_wb()
                    kb.op("pe", lambda e, wbt=wbt, hh=hh: e.matmul(wbt[:, :], lhsT=j32[:, :], rhs=hk[:, hh * 4:(hh + 1) * 4, :], start=True, stop=True),
                          r=[B_j32, B_hk], w=[wbb])
                    kb.op("dve", lambda e, wbt=wbt, hh=hh, dst=dst: e.tensor_copy(out=dst[:, hh * 4:(hh + 1) * 4, :], in_=wbt[:, :].rearrange("p (a b) -> p a b", a=4)),
                          r=[wbb], w=[Bdst])
            kb.op("dve", lambda e: e.tensor_copy(out=GN[:, :, 0:8], in_=T1[:, :, 15:128:16]), r=[B_T1], w=[B_GN])
            kb.op("dve", lambda e: e.tensor_copy(out=GN[:, :, 8:16], in_=T0[:, :, 15:128:16]), r=[B_T0], w=[B_GN])
            if do_sample:
                pti, B_pti = sb2("pti", [128, 256], I32)
                ptf, B_ptf = sb2("ptf", [128, 256], F32)
                iop, B_iop = sb2("iop", [128, 1], F32)
                kb.dma("sp", pti[:], bass.AP(tensor=ptab, offset=0, ap=[[0, 128], [1, 256]]), w=[B_pti])
                kb.dma("sp", iop[:], c_iota.ap(), w=[B_iop])
                kb.op("dve", lambda e: e.tensor_copy(out=ptf[:], in_=pti[:]), r=[B_pti], w=[B_ptf])
                kb.op("dve", lambda e: e.tensor_scalar(out=ptf[:], in0=ptf[:], scalar1=128.0, scalar2=iop[:, 0:1], op0=ALU.mult, op1=ALU.add), r=[B_ptf, B_iop], w=[B_ptf])
                kb.op("dve", lambda e: e.tensor_copy(out=IDX[:], in_=ptf[:]), r=[B_ptf], w=[B_IDX])
                rhsF, B_rhsF = sb2("rhsF", [9, 64], F32)
                selb, B_selb = sb2("selb", [9, 16, 128], F32)
                kb.dma("sp", selb[:], c_selb.ap(), w=[B_selb])
                kb.dma("sp", rhsF[8:9, :], c_neg64.ap(), w=[B_rhsF])
                kb.op("dve", lambda e: e.tensor_copy(out=rhsF[0:8, :].rearrange("p (h q) -> p h q", h=8), in_=F0[0:8, :, 0:8]), r=[B_F0], w=[B_rhsF])
                for b in range(16):
                    wbt, wbb = next_wb()
                    kb.op("pe", lambda e, wbt=wbt, b=b: e.matmul(wbt[:, 0:64], lhsT=selb[:, b, :], rhs=rhsF[:, :], start=True, stop=True), r=[B_selb, B_rhsF], w=[wbb])
                    kb.op("dve", lambda e, wbt=wbt, b=b: e.tensor_copy(out=FSB[:, b, :], in_=wbt[:, 0:64]), r=[wbb], w=[B_FSB])
            kb.barrier()
            if stage == 1:
                kb.dma("sp", y_p[0:128, 0:1024], F0[:], r=[B_F0], final=True)
                kb.dma("sp", y_p[128:256, 0:1024], F1[:], r=[B_F1], final=True)
                kb.dma("sp", y_p[256:384, 0:1024], FW[:], r=[B_FW], final=True)
                kb.dma("sp", y_p[384:512, 0:128], GN[:], r=[B_GN], final=True)
                kb.emit()
                return nc

        for l in range(L):
            last = (l == L - 1)
            with contextlib.ExitStack() as stA:
                def sa(name, shape, dt):
                    return stA.enter_context(nc.sbuf_tensor(un(name), list(shape), dt)), Buf(name)
                w_in_sb, B_win = sa("w_in_sb", [128, 8, INC], BF16)
                w_out_sb, B_wout = sa("w_out_sb", [128, 8, D], BF16)
                gmix, B_gmix = sa("gmix", [128, D], F32)
                wk_bd, B_wk = sa("wk_bd", [128, 32, 128], BF16)
                wv_bd, B_wv = sa("wv_bd", [128, 32, 128], BF16)
                peT, B_peT = sa("peT", [128, 2, 32], BF16)
                biasKV, B_bkv = sa("biasKV", [128, 2], F32)
                wpool_sb, B_wpool = sa("wpool_sb", [128, 4, 128], BF16)
                pscale, B_pscale = sa("pscale", [128, 4], F32)
                KE = [sa("KE%d" % i, [128, TE], BF16) for i in range(2)]
                Vsel, B_Vsel = sa("Vsel", [128, max(NT, 16), 2, 65], BF16)
                KTwin, B_KTwin = sa("KTwin", [128, 8, 128], BF16)
                Vwin, B_Vwin = sa("Vwin", [128, 8, 2, 65], BF16)
                KcT, B_KcT = sa("KcT", [128, 144], BF16)
                VcT, B_VcT = sa("VcT", [128, 144], BF16)
                compKT, B_cKT = sa("compKT", [128, 256], BF16)
                compVT, B_cVT = sa("compVT", [128, 256], BF16)
                compV, B_cV = sa("compV", [128, 2, 128], BF16)
                xt = [sa("xt%d" % i, [128, D], F32) for i in range(2)]
                ss, B_ss = sa("ss", [128, 2], F32)
                hb2 = [sa("h_bf%d" % i, [128, D], BF16) for i in range(2)]
                hT2 = [sa("hT%d" % i, [128, 8, 128], BF16) for i in range(2)]
                ssl = [sa("ssn%d" % i, [128, 2], F32) for i in range(2)]
                QN = [sa("QN%d" % i, [128, 4, 128], BF16) for i in range(2)]
                QZ = [sa("QZ%d" % i, [128, 4, 128], BF16) for i in range(2)]
                kvrow, B_kv = sa("kvrow", [128, 768], F32)
                gates, B_gates = sa("gates", [128, 24], F32)
                u_f, B_uf = sa("u_f", [128, 512], F32)
                u_bf = [sa("u_bf%d" % i, [128, 512], BF16) for i in range(2)]
                Scmp2 = [sa("Scmp%d" % i, [128, 4, 256], F32) for i in range(2)]
                rsum2 = [sa("rsum%d" % i, [128, 8], F32) for i in range(2)]
                Pn2 = [sa("Pn%d" % i, [128, 4, 256], BF16) for i in range(2)]
                Psum_g, B_Psg = sa("Psum_g", [128, 264], F32)
                imp, B_imp = sa("imp", [128, 64], F32)
                wk64, B_wk64 = sa("wk64", [128, 64], F32)
                m8, B_m8 = sa("m8", [128, 16], F32)
                NMq2 = [sa("NMq%d" % i, [128, 128], BF16) for i in range(2)]
                PTc, B_PTc = sa("PTc", [128, 2, 4, 128], BF16)
                Sadd = [sa("Sadd%d" % i, [128, 512], F32) for i in range(2)]
                PT = [sa("PT%d" % i, [128, 512], BF16) for i in range(3)]
                o_att, B_oatt = sa("o_att", [128, 512], F32)
                o_tmp, B_otmp = sa("o_tmp", [128, 512], F32)
                coef, B_coef = sa("coef", [128, 8], F32)
                o_bf, B_obf = sa("o_bf", [128, 512], BF16)
                catT, B_catT = sa("catT", [128, 8, 128], BF16)
                poolT, B_poolT = sa("poolT", [128, 4, 128], BF16)
                xnew = [sa("xnew%d" % i, [128, D], F32) for i in range(1)]
                if do_sample:
                    Gb = [sa("Gb%d" % i, [128, 8, 256], BF16) for i in range(2)]
                    VsT, B_VsT = sa("VsT", [128, 2048], BF16)
                    KsT, B_KsT = sa("KsT", [128, 2048], BF16)
                    KnT, B_KnT = sa("KnT", [128, 128], BF16)
                    Vn, B_Vn = sa("Vn", [128, 2, 2, 65], BF16)
                    o_att_s, B_oatts = sa("o_att_s", [128, 512], F32)
                    gates_s, B_gates_s = sa("gates_s", [8, 16, 24], F32)
                    sp_bf, B_spbf = sa("sp_bf", [128, 2, 512], BF16)
                if stage == 98:
                    print("sbuf bytes remaining after phase-AB allocs:", nc.sbuf_bytes_remaining)
                rot2 = {"sadd": 0, "pt": 0}

                for k in range(8):
                    for g in range(2):
                        kb.dma("pool", w_in_sb[:, k, 0:512].rearrange("r (p g d) -> r p g d", p=4, g=2)[:, :, g, :],
                               w_in[l, k * 128:(k + 1) * 128, g * 256:(g + 1) * 256].rearrange("r (p d) -> r p d", p=4), w=[B_win])
                    kb.dma("pool", w_in_sb[:, k, 512:INC], w_in[l, k * 128:(k + 1) * 128, 512:INC], w=[B_win])
                    kb.dma("pool", w_out_sb[:, k, :], w_out[l, k * 128:(k + 1) * 128, :], w=[B_wout])
                kb.dma("sp", gmix[:], bass.AP(tensor=norm_mix, offset=l * D, ap=[[0, 128], [1, D]]), w=[B_gmix])
                kb.op("pool", lambda e: e.memset(wk_bd[:], 0.0), w=[B_wk])
                kb.op("pool", lambda e: e.memset(wv_bd[:], 0.0), w=[B_wv])
                for g in range(2):
                    kb.dma("pool", wk_bd[64 * g:64 * g + 64, :, 64 * g:64 * g + 64], w_cmp[l, 0].rearrange("l d e -> d l e"), w=[B_wk])
                    kb.dma("pool", wv_bd[64 * g:64 * g + 64, :, 64 * g:64 * g + 64], w_cmp[l, 1].rearrange("l d e -> d l e"), w=[B_wv])
                    for j in range(2):
                        kb.dma("pool", peT[64 * g:64 * g + 64, j, :], cmp_pos[l, :, j, :].rearrange("l d -> d l"), w=[B_peT],
                               allow_slow_non_contiguous=True)
                kb.dma("pool", wpool_sb[:], w_pool[l].rearrange("k c d -> c k d"), w=[B_wpool])
                kb.dma("sp", pscale[:], pool_scale[l].rearrange("(k d) -> d k", k=4), w=[B_pscale], allow_slow_non_contiguous=True)
                for j, (wbd, Bw) in enumerate(((wk_bd, B_wk), (wv_bd, B_wv))):
                    wbt, wbb = next_wb()
                    for ll in range(32):
                        kb.op("pe", lambda e, wbt=wbt, wbd=wbd, ll=ll, j=j: e.matmul(wbt[:, 0:1], lhsT=wbd[:, ll, :], rhs=peT[:, j, ll:ll + 1], start=(ll == 0), stop=(ll == 31)),
                              r=[Bw, B_peT], w=[wbb])
                    kb.op("dve", lambda e, wbt=wbt, j=j: e.tensor_copy(out=biasKV[:, j:j + 1], in_=wbt[:, 0:1]), r=[wbb], w=[B_bkv])
                kb.op("pool", lambda e: e.memset(Vsel[:, :, :, 64:65], 1.0), w=[B_Vsel])
                kb.op("pool", lambda e: e.memset(Vwin[:, :, :, 64:65], 1.0), w=[B_Vwin])
                kb.op("pool", lambda e: e.memset(compVT[:], 0.0), w=[B_cVT])
                kb.op("pool", lambda e: e.memset(compKT[:], 0.0), w=[B_cKT])
                kb.op("pool", lambda e: e.memset(compV[:], 0.0), w=[B_cV])
                kb.op("pool", lambda e: e.memset(KcT[:], 0.0), w=[B_KcT])
                kb.op("pool", lambda e: e.memset(VcT[:], 0.0), w=[B_VcT])
                kb.op("pool", lambda e: e.memset(Psum_g[:], 0.0), w=[B_Psg])
                for g in range(2):
                    kb.op("pool", lambda e: e.memset(NMq2[g][0][:], 0.0), w=[NMq2[g][1]])
                    kb.op("pool", lambda e: e.memset(QZ[g][0][:], 0.0), w=[QZ[g][1]])
                    kb.dma("sp", KE[g][0][64:128, :], c_ex.ap(), w=[KE[g][1]])

                if stage == 2:
                    kb.dma("sp", y_p[0:128, 0:2], biasKV[:], r=[B_bkv], final=True)
                    kb.emit()
                    return nc

                def prep_norm(xsrc_ap, B_src, par):
                    xtt, B_xt = xt[par]
                    h_bf, B_h = hb2[par]
                    ss, B_ss = ssl[par]
                    kb.dma("sp", xtt[:], xsrc_ap, r=[B_src] if B_src is not None else [], w=[B_xt])
                    kb.op("act", lambda e: e.activation(out=h_bf[:], in_=xtt[:], func=AF.Square, accum_out=ss[:, 0:1]), r=[B_xt], w=[B_h, B_ss])
                    kb.op("dve", lambda e: e.tensor_scalar(out=ss[:, 1:2], in0=ss[:, 0:1], scalar1=1.0 / D, scalar2=EPS, op0=ALU.mult, op1=ALU.add), r=[B_ss], w=[B_ss])
                    kb.op("act", lambda e: e.activation(out=ss[:, 1:2], in_=ss[:, 1:2], func=AF.Sqrt), r=[B_ss], w=[B_ss])
                    kb.op("dve", lambda e: e.reciprocal(out=ss[:, 1:2], in_=ss[:, 1:2]), r=[B_ss], w=[B_ss])
                    kb.op("dve", lambda e: e.scalar_tensor_tensor(out=h_bf[:], in0=xtt[:], scalar=ss[:, 1:2], in1=gmix[:], op0=ALU.mult, op1=ALU.mult),
                          r=[B_xt, B_ss, B_gmix], w=[B_h])

                def prep_T(par):
                    h_bf, B_h = hb2[par]
                    hT, B_hT = hT2[par]
                    tbt, tbb = next_tb()
                    for k in range(8):
                        kb.op("pe", lambda e, k=k: e.transpose(out=tbt[:, k * 128:(k + 1) * 128], in_=h_bf[:, k * 128:(k + 1) * 128], identity=ident[:]),
                              r=[B_h, B_ident], w=[tbb])
                    kb.op("act", lambda e: e.copy(out=hT[:].rearrange("p a b -> p (a b)"), in_=tbt[:, :]), r=[tbb], w=[B_hT])

                def project(par, ksel_col):
                    hT, B_hT = hT2[par]
                    wbt, wbb = next_wb()
                    for p in range(4):
                        for k in range(8):
                            lhs = w_in_sb[:, k, p * 128:(p + 1) * 128]
                            kb.op("pe", lambda e, lhs=lhs, p=p, k=k: e.matmul(wbt[:, p * 128:(p + 1) * 128], lhsT=lhs, rhs=hT[:, k, :], start=(k == 0), stop=(k == 7)),
                                  r=[B_win, B_hT], w=[wbb])
                    kb.op("act", lambda e: e.activation(out=QZ[0][0][0:64, :, :].rearrange("p a b -> p (a b)"), in_=wbt[0:64, :], func=AF.Copy, scale=0.125), r=[wbb], w=[QZ[0][1]])
                    kb.op("act", lambda e: e.activation(out=QZ[1][0][64:128, :, :].rearrange("p a b -> p (a b)"), in_=wbt[64:128, :], func=AF.Copy, scale=0.125), r=[wbb], w=[QZ[1][1]])
                    kb.op("dve", lambda e: e.tensor_scalar(out=QN[0][0][0:64, :, :].rearrange("p a b -> p (a b)"), in0=wbt[0:64, :], scalar1=0.125, scalar2=None, op0=ALU.mult), r=[wbb], w=[QN[0][1]])
                    wbx, wbxb = next_wb()
                    for p in range(4):
                        for k in range(8):
                            kb.op("pe", lambda e: e.matmul(wbx[0:64, p * 128:(p + 1) * 128], lhsT=w_in_sb[:, k, p * 128 + 64:(p + 1) * 128], rhs=hT[:, k, :], start=(k == 0), stop=(k == 7)),
                                  r=[B_win, B_hT], w=[wbxb])
                    kb.op("act", lambda e: e.activation(out=QN[1][0][0:64, :, :].rearrange("p a b -> p (a b)"), in_=wbx[0:64, :], func=AF.Copy, scale=0.125), r=[wbxb], w=[QN[1][1]])
                    wby, wbyb = next_wb()
                    for g in range(2):
                        for k in range(8):
                            kb.op("pe", lambda e: e.matmul(wby[0:64, g * 128:(g + 1) * 128], lhsT=w_in_sb[:, k, 768 + 64 * g:768 + 64 * g + 64], rhs=hT[:, k, :], start=(k == 0), stop=(k == 7)),
                                  r=[B_win, B_hT], w=[wbyb])
                    kb.op("dve", lambda e: e.tensor_copy(out=KE[0][0][0:64, ksel_col:ksel_col + 128], in_=wby[0:64, 0:128]), r=[wbyb], w=[KE[0][1]])
                    kb.op("act", lambda e: e.copy(out=KE[1][0][0:64, ksel_col:ksel_col + 128], in_=wby[0:64, 128:256]), r=[wbyb], w=[KE[1][1]])
                    wbt2, wbb2 = next_wb()
                    for i, c0 in ((0, 512), (2, 1024), (3, 640)):
                        for k in range(8):
                            kb.op("pe", lambda e, i=i, c0=c0, k=k: e.matmul(wbt2[:, i * 128:(i + 1) * 128], lhsT=w_in_sb[:, k, c0:c0 + 128], rhs=hT[:, k, :], start=(k == 0), stop=(k == 7)),
                                  r=[B_win, B_hT], w=[wbb2])
                    zc = []
                    for (c0, c1) in ((512, 1024), (1024, 1304), (1304, 1816)):
                        zt, zb = next_wb()
                        for k in range(8):
                            kb.op("pe", lambda e, zt=zt, c0=c0, c1=c1, k=k: e.matmul(zt[:, 0:c1 - c0], lhsT=hT[:, k, :], rhs=w_in_sb[:, k, c0:c1], start=(k == 0), stop=(k == 7)),
                                  r=[B_win, B_hT], w=[zb])
                        zc.append((zt, zb))
                    return wbt2, wbb2, zc

                def evac_tokmajor(zc, ub):
                    (zA, bA), (zB, bB), (zC, bC) = zc
                    kb.op("act", lambda e: e.copy(out=kvrow[:, 0:512], in_=zA[:, 0:512]), r=[bA], w=[B_kv])
                    kb.op("dve", lambda e: e.tensor_copy(out=kvrow[:, 512:768], in_=zB[:, 0:256]), r=[bB], w=[B_kv])
                    kb.op("act", lambda e: e.activation(out=gates[:], in_=zB[:, 256:280], func=AF.Sigmoid), r=[bB], w=[B_gates])
                    kb.op("dve", lambda e: e.tensor_copy(out=u_f[:], in_=zC[:, 0:512]), r=[bC], w=[B_uf])
                    kb.op("act", lambda e: e.copy(out=ub[0][:], in_=zC[:, 0:512]), r=[bC], w=[ub[1]])

                def attend(nq, qsl, key_tiles, Bo, g, first_flag, fsl=None):
                    ot, ob = OB[g]
                    n = 4 * nq
                    if fsl is None:
                        fsl = qsl
                    ntl = len(key_tiles)

                    def stage_a(ti):
                        kt_ap, v_ap, nk, Fb, mask, bufs = key_tiles[ti]
                        wbt, wbb = next_wb()
                        qsrc, qb = (QN[g] if mask else QZ[g])
                        rhs = qsrc[:, :, qsl]
                        kb.op("pe", lambda e: e.matmul(wbt[0:nk, 0:n].rearrange("p (a b) -> p a b", a=4), lhsT=kt_ap, rhs=rhs, start=True, stop=True),
                              r=[qb] + bufs, w=[wbb])
                        if Fb is not None:
                            Ft, FB_ = Fb
                            rot2["sadd"] += 1
                            sat, sab = Sadd[rot2["sadd"] % 2]
                            kb.op("dve", lambda e: e.tensor_tensor(out=sat[0:nk, 0:n].rearrange("p (a b) -> p a b", a=4), in0=wbt[0:nk, 0:n].rearrange("p (a b) -> p a b", a=4),
                                                                   in1=Ft[0:nk, 4 * g:4 * g + 4, fsl], op=ALU.add),
                                  r=[wbb, FB_], w=[sab])
                            return sat, sab
                        return wbt, wbb

                    def stage_b(ti, src):
                        kt_ap, v_ap, nk, Fb, mask, bufs = key_tiles[ti]
                        st_t, st_b = src
                        rot2["pt"] += 1
                        ptt, ptb = PT[rot2["pt"] % 3]
                        kb.op("act", lambda e: e.activation(out=ptt[0:nk, 0:n], in_=st_t[0:nk, 0:n], func=AF.Exp), r=[st_b], w=[ptb])
                        for h in range(4):
                            st_ = first_flag[0]
                            first_flag[0] = False
                            kb.op("pe", lambda e: e.matmul(ot[0:nq, h * 65:(h + 1) * 65], lhsT=ptt[0:nk, h * nq:(h + 1) * nq], rhs=v_ap, start=st_, stop=(ti == ntl - 1), skip_group_check=True),
                                  r=[ptb] + bufs, w=[ob])

                    pend = []
                    for ti in range(ntl):
                        pend.append((ti, stage_a(ti)))
                        if len(pend) > 2:
                            stage_b(*pend.pop(0))
                    while pend:
                        stage_b(*pend.pop(0))

                def combine(nq, g, br, first, gsrc=None):
                    ot, ob = OB[g]
                    ov = ot[0:nq, 0:260].rearrange("p (h d) -> p h d", h=4)
                    cf = coef[0:nq, 4 * g:4 * g + 4]
                    gt_ap, B_gt = gsrc if gsrc is not None else (gates[0:nq, :], B_gates)
                    gv = gt_ap.rearrange("p (h b) -> p h b", b=3)[:, 4 * g:4 * g + 4, br]
                    if br == 0:
                        kb.op("dve", lambda e: e.tensor_copy(out=cf, in_=gv), r=[B_gt], w=[B_coef])
                    else:
                        kb.op("dve", lambda e: e.reciprocal(out=cf, in_=ov[:, :, 64]), r=[ob], w=[B_coef])
                        kb.op("dve", lambda e: e.tensor_tensor(out=cf, in0=cf, in1=gv, op=ALU.mult), r=[B_coef, B_gt], w=[B_coef])
                    first = False
                    dst = o_tmp[0:nq, 256 * g:256 * g + 256].rearrange("p (h d) -> p h d", h=4)
                    Bd = B_otmp
                    kb.op("dve", lambda e: e.tensor_tensor(out=dst, in0=ov[:, :, 0:64], in1=bc_ap(cf, [64]), op=ALU.mult), r=[ob, B_coef], w=[Bd])
                    if not first:
                        oa = o_att[0:nq, 256 * g:256 * g + 256]
                        kb.op("dve", lambda e: e.tensor_tensor(out=oa, in0=oa, in1=o_tmp[0:nq, 256 * g:256 * g + 256], op=ALU.add), r=[B_oatt, B_otmp], w=[B_oatt])

                def cmp_1a(nq, qsl, ncv, gn_lo, gn_hi, g):
                    Scmp, B_Scmp = Scmp2[g]
                    rsum, B_rsum = rsum2[g]
                    for hp in range(2):
                        wbt, wbb = next_wb()
                        for hh in range(2):
                            h = hp * 2 + hh
                            kb.op("pe", lambda e, wbt=wbt, h=h, hh=hh: e.matmul(wbt[0:nq, hh * 256:hh * 256 + ncv], lhsT=QZ[g][0][:, h, qsl], rhs=compKT[:, 0:ncv], start=(hh == 0), stop=(hh == 1), skip_group_check=True),
                                  r=[QZ[g][1], B_cKT], w=[wbb])
                        ngn = gn_hi - gn_lo
                        wv = wbt[0:nq, :].rearrange("p (a b) -> p a b", a=2)
                        kb.op("dve", lambda e, wv=wv, hp=hp: e.tensor_tensor(out=wv[:, :, ncv - ngn:ncv], in0=wv[:, :, ncv - ngn:ncv], in1=GN[0:nq, 4 * g + 2 * hp:4 * g + 2 * hp + 2, gn_lo:gn_hi], op=ALU.add),
                              r=[wbb, B_GN], w=[wbb])
                        for hh in range(2):
                            h = hp * 2 + hh
                            kb.op("act", lambda e, wbt=wbt, h=h, hh=hh: e.activation(out=Scmp[0:nq, h, 0:ncv], in_=wbt[0:nq, hh * 256:hh * 256 + ncv], func=AF.Exp, accum_out=rsum[0:nq, h:h + 1]),
                                  r=[wbb], w=[B_Scmp, B_rsum])

                def cmp_1b(nq, ncv, t_fc, g):
                    Pn, B_Pn = Pn2[g]
                    Scmp, B_Scmp = Scmp2[g]
                    rsum, B_rsum = rsum2[g]
                    kb.op("dve", lambda e: e.tensor_scalar_max(out=rsum[0:nq, 0:4], in0=rsum[0:nq, 0:4], scalar1=1e-30), r=[B_rsum], w=[B_rsum])
                    kb.op("dve", lambda e: e.reciprocal(out=rsum[0:nq, 4:8], in_=rsum[0:nq, 0:4]), r=[B_rsum], w=[B_rsum])
                    kb.op("dve", lambda e: e.tensor_tensor(out=Scmp[0:nq, :, 0:ncv], in0=Scmp[0:nq, :, 0:ncv], in1=bc_ap(rsum[0:nq, 4:8], [ncv]), op=ALU.mult),
                          r=[B_Scmp, B_rsum], w=[B_Scmp])
                    kb.op("dve", lambda e: e.tensor_tensor(out=Psum_g[0:nq, 1:1 + ncv], in0=Scmp[0:nq, 0, 0:ncv], in1=Scmp[0:nq, 1, 0:ncv], op=ALU.add), r=[B_Scmp], w=[B_Psg])
                    for h in (2, 3):
                        kb.op("dve", lambda e, h=h: e.tensor_tensor(out=Psum_g[0:nq, 1:1 + ncv], in0=Psum_g[0:nq, 1:1 + ncv], in1=Scmp[0:nq, h, 0:ncv], op=ALU.add), r=[B_Scmp, B_Psg], w=[B_Psg])
                    kb.op("dve", lambda e: e.tensor_tensor(out=imp[0:nq, :], in0=Psum_g[0:nq, 0:256:4], in1=Psum_g[0:nq, 1:257:4], op=ALU.add), r=[B_Psg], w=[B_imp])
                    for o in (2, 3, 4):
                        kb.op("dve", lambda e, o=o: e.tensor_tensor(out=imp[0:nq, :], in0=imp[0:nq, :], in1=Psum_g[0:nq, o:o + 256:4], op=ALU.add), r=[B_Psg, B_imp], w=[B_imp])
                    kb.op("dve", lambda e: e.tensor_tensor(out=imp[0:nq, :], in0=imp[0:nq, :], in1=FC[0:nq, 64 - 2 * t_fc:128 - 2 * t_fc], op=ALU.add), r=[B_imp, B_FC], w=[B_imp])
                    kb.op("dve", lambda e: e.tensor_copy(out=imp[0:nq, 0:1], in_=big9[0:nq, :]), r=[B_imp, B_big9], w=[B_imp])
                    nround = (NSEL + 7) // 8
                    src = imp
                    for rd in range(nround):
                        kb.op("dve", lambda e, src=src, rd=rd: e.max(out=m8[0:nq, rd * 8:rd * 8 + 8], in_=src[0:nq, :]), r=[B_imp, B_wk64], w=[B_m8])
                        if rd < nround - 1:
                            kb.op("dve", lambda e, src=src, rd=rd: e.match_replace(out=wk64[0:nq, :], in_to_replace=m8[0:nq, rd * 8:rd * 8 + 8], in_values=src[0:nq, :], imm_value=-3e9),
                                  r=[B_m8, B_imp, B_wk64], w=[B_wk64])
                            src = wk64
                    kb.op("dve", lambda e: e.tensor_tensor(out=wk64[0:nq, :], in0=imp[0:nq, :], in1=bc_ap(m8[0:nq, NSEL - 1:NSEL], [])[:, 0:1].to_broadcast([nq, 64]) if False else bass.AP(tensor=m8[:].tensor, offset=m8[0:nq, NSEL - 1:NSEL].offset, ap=[list(m8[0:nq, :].ap[0]), [0, 64]]), op=ALU.is_ge), r=[B_imp, B_m8], w=[B_wk64])
                    kb.op("dve", lambda e: e.tensor_scalar(out=NMq2[g][0][0:nq, 64:128], in0=wk64[0:nq, :], scalar1=-1.0, scalar2=-NEG, op0=ALU.add, op1=ALU.mult), r=[B_wk64], w=[NMq2[g][1]])
                    kb.op("act", lambda e: e.copy(out=Pn[0:nq, :, 0:ncv], in_=Scmp[0:nq, :, 0:ncv]), r=[B_Scmp], w=[B_Pn])

                def cmp_part2(nq, qsl, ncv, g, gsrc=None):
                    Pn, B_Pn = Pn2[g]
                    tbt, tbb = next_tb()
                    kb.op("pe", lambda e: e.transpose(out=tbt[0:128, 0:nq], in_=NMq2[g][0][0:nq, :], identity=ident[0:nq, 0:nq]), r=[NMq2[g][1], B_ident], w=[tbb])
                    tsrc = tbt[64:128, 0:nq]
                    kb.op("act", lambda e: e.copy(out=QN[g][0][64:128, :, qsl], in_=bass.AP(tensor=tsrc.tensor, offset=tsrc.offset, ap=[list(tsrc.ap[0]), [0, 4], [1, nq]])), r=[tbb], w=[QN[g][1]])
                    nct = (ncv + 127) // 128
                    ot, ob = OB[g]
                    for ct in range(nct):
                        cw = min(128, ncv - ct * 128)
                        tbt, tbb = next_tb()
                        for h in range(4):
                            kb.op("pe", lambda e, tbt=tbt, h=h, ct=ct, cw=cw: e.transpose(out=tbt[0:cw, h * 128:h * 128 + nq], in_=Pn[0:nq, h, ct * 128:ct * 128 + cw], identity=ident[0:nq, 0:nq]),
                                  r=[B_Pn, B_ident], w=[tbb])
                        kb.op("act", lambda e, tbt=tbt, ct=ct, cw=cw: e.copy(out=PTc[0:cw, ct, :, 0:nq], in_=tbt[0:cw, 0:512].rearrange("p (a b) -> p a b", a=4)[:, :, 0:nq]),
                              r=[tbb], w=[B_PTc])
                    first = True
                    for ct in range(nct):
                        cw = min(128, ncv - ct * 128)
                        for h in range(4):
                            kb.op("pe", lambda e, h=h, ct=ct, cw=cw, first=first, lastk=(ct == nct - 1): e.matmul(ot[0:nq, h * 65:h * 65 + 64], lhsT=PTc[0:cw, ct, h, 0:nq], rhs=compV[0:cw, ct, 64 * g:64 * g + 64], start=first, stop=lastk, skip_group_check=True),
                                  r=[B_PTc, B_cV], w=[ob])
                            first = False
                    combine(nq, g, 0, True, gsrc)

                def finish_tile(nq, xtt, B_xt, par, ub_cur, prev_pool, kinds, xdst_ap, B_xdst, oa=None):
                    oat, B_oat = oa if oa is not None else (o_att, B_oatt)
                    kb.op("pool", lambda e: e.tensor_copy(out=o_bf[:], in_=oat[:]), r=[B_oat], w=[B_obf])
                    tbt, tbb = next_tb()
                    for k in range(4):
                        kb.op("pe", lambda e, k=k: e.transpose(out=tbt[:, k * 128:(k + 1) * 128], in_=o_bf[:, k * 128:(k + 1) * 128], identity=ident[:]), r=[B_obf, B_ident], w=[tbb])
                    kb.op("act", lambda e: e.copy(out=catT[:, 0:4, :].rearrange("p a b -> p (a b)"), in_=tbt[:, 0:512]), r=[tbb], w=[B_catT])
                    wbt, wbb = next_wb()
                    for k in range(4):
                        srcs = [(ub_cur[0][:, k * 128:(k + 1) * 128], PB[:, kinds[0] * 4 + k, :], ub_cur[1])]
                        for (pa, pk, pbuf) in prev_pool:
                            srcs.append((pa[:, k * 128:(k + 1) * 128], PB[:, pk * 4 + k, :], pbuf))
                        for si, (la, ra, bb) in enumerate(srcs):
                            kb.op("pe", lambda e, la=la, ra=ra, k=k, si=si, ns=len(srcs): e.matmul(wbt[:, k * 128:(k + 1) * 128], lhsT=la, rhs=ra, start=(si == 0), stop=(si == ns - 1)),
                                  r=[bb, B_PB], w=[wbb])
                    kb.op("act", lambda e: e.copy(out=poolT[:].rearrange("p a b -> p (a b)"), in_=wbt[:, :]), r=[wbb], w=[B_poolT])
                    wbt2, wbb2 = next_wb()
                    for k in range(4):
                        kb.op("pe", lambda e, k=k: e.matmul(wbt2[:, k * 128:(k + 1) * 128], lhsT=wpool_sb[:, k, :], rhs=poolT[:, k, :], start=True, stop=True), r=[B_wpool, B_poolT], w=[wbb2])
                    for k in range(4):
                        kb.op("act", lambda e, k=k: e.activation(out=catT[:, 4 + k, :], in_=wbt2[:, k * 128:(k + 1) * 128], func=AF.Identity, scale=pscale[:, k:k + 1]), r=[wbb2, B_pscale], w=[B_catT])
                    xn, B_xn = xnew[0]
                    for hf in range(2):
                        wo, wob = next_wb()
                        for k in range(8):
                            kb.op("pe", lambda e, wo=wo, k=k, hf=hf: e.matmul(wo[:, :], lhsT=catT[:, k, :], rhs=w_out_sb[:, k, hf * 512:(hf + 1) * 512], start=(k == 0), stop=(k == 7)),
                                  r=[B_catT, B_wout], w=[wob])
                        kb.op("dve", lambda e, wo=wo, hf=hf: e.tensor_tensor(out=xn[:, hf * 512:(hf + 1) * 512], in0=wo[:, :], in1=xtt[:, hf * 512:(hf + 1) * 512], op=ALU.add), r=[wob, B_xt], w=[B_xn])
                    kb.dma("sp", xdst_ap, xn[:], r=[B_xn], w=[B_xdst])

                def prep_for(tt):
                    if tt < NT:
                        return ((xp if l == 0 else xscr_p)[tt * 128:(tt + 1) * 128, :], None if l == 0 else B_xscr_p[tt])
                    if tt == NT and do_sample:
                        return ((xs if l == 0 else xscr_s)[:, :], None if l == 0 else B_xscr_s)
                    return None

                a0 = prep_for(0)
                prep_norm(a0[0], a0[1], 0)
                prep_T(0)
                for t in range(NT):
                    par = t % 2
                    fk, fkb, zc = project(par, t * 128)
                    nxt = prep_for(t + 1)
                    if nxt is not None:
                        prep_norm(nxt[0], nxt[1], 1 - par)
                    xtt, B_xt = xt[par]
                    ring = t % 8
                    kb.op("act", lambda e: e.copy(out=KTwin[:, ring, :], in_=fk[:, 256:384]), r=[fkb], w=[B_KTwin])
                    kb.op("dve", lambda e: e.tensor_copy(out=KcT[:, 16:144], in_=fk[:, 0:128]), r=[fkb], w=[B_KcT])
                    kb.op("act", lambda e: e.copy(out=VcT[:, 16:144], in_=fk[:, 384:512]), r=[fkb], w=[B_VcT])
                    evac_tokmajor(zc, u_bf[par])
                    if stage == 3:
                        kb.dma("sp", ncmp_p[l, t * 128:(t + 1) * 128, :], kvrow[:, 0:256], r=[B_kv], final=True)
                        kb.emit()
                        return nc
                    kb.op("pool", lambda e: e.tensor_copy(out=Vsel[:, t, :, 0:64], in_=kvrow[:, 384:512].rearrange("p (g d) -> p g d", g=2)), r=[B_kv], w=[B_Vsel])
                    kb.op("pool", lambda e: e.tensor_copy(out=Vwin[:, ring, :, 0:64], in_=kvrow[:, 640:768].rearrange("p (g d) -> p g d", g=2)), r=[B_kv], w=[B_Vwin])
                    kb.dma("sp", ncmp_p[l, t * 128:(t + 1) * 128, :], kvrow[:, 0:256], r=[B_kv], final=True)
                    kb.dma("sp", nsel_p[l, t * 128:(t + 1) * 128, :], kvrow[:, 256:512], r=[B_kv], final=True)
                    if t >= NT - 4:
                        kb.dma("sp", nwin_p[l, (t - (NT - 4)) * 128:(t - (NT - 4) + 1) * 128, :], kvrow[:, 512:768], r=[B_kv], final=True)
                    if t == NT - 1:
                        kb.dma("sp", npool_p[l, :, :], u_f[113:128, :], r=[B_uf], final=True)
                    if do_attn:
                        i0 = 1 if t == 0 else 0
                        c0 = 8 * t - 1 + i0
                        ncn = 8 - i0
                        for (srcT, Bs, wbd, Bw, dstT, Bd, j) in ((KcT, B_KcT, wk_bd, B_wk, compKT, B_cKT, 0), (VcT, B_VcT, wv_bd, B_wv, compVT, B_cVT, 1)):
                            wbt, wbb = next_wb()
                            for ll in range(32):
                                kb.op("pe", lambda e, wbt=wbt, wbd=wbd, srcT=srcT, ll=ll: e.matmul(wbt[:, 0:ncn], lhsT=wbd[:, ll, :], rhs=srcT[:, ll + 16 * i0:ll + 16 * i0 + 16 * (ncn - 1) + 1:16], start=(ll == 0), stop=(ll == 31)),
                                      r=[Bw, Bs], w=[wbb])
                            kb.op("act", lambda e, wbt=wbt, dstT=dstT, j=j: e.activation(out=dstT[:, c0:c0 + ncn], in_=wbt[:, 0:ncn], func=AF.Identity, bias=biasKV[:, j:j + 1]), r=[wbb, B_bkv], w=[Bd])
                        kb.op("dve", lambda e: e.tensor_copy(out=KcT[:, 0:16], in_=KcT[:, 128:144]), r=[B_KcT], w=[B_KcT])
                        kb.op("dve", lambda e: e.tensor_copy(out=VcT[:, 0:16], in_=VcT[:, 128:144]), r=[B_VcT], w=[B_VcT])
                        if nxt is not None:
                            prep_T(1 - par)
                        for ct in sorted(set((c0 // 128, (c0 + ncn - 1) // 128))):
                            tbt, tbb = next_tb()
                            kb.op("pe", lambda e, tbt=tbt, ct=ct: e.transpose(out=tbt[:, 0:128], in_=compVT[:, ct * 128:(ct + 1) * 128], identity=ident[:]), r=[B_cVT, B_ident], w=[tbb])
                            kb.op("dve", lambda e, tbt=tbt, ct=ct: e.tensor_copy(out=compV[:, ct, :], in_=tbt[:, 0:128]), r=[tbb], w=[B_cV])
                        ncv = 8 * t + 7
                        qsl = slice(0, 128)
                        kb.op("pool", lambda e: e.memset(o_att[:], 0.0), w=[B_oatt])
                        for g in range(2):
                            cmp_1a(128, qsl, ncv, max(0, 9 - 8 * t), 16, g)
                        for g in range(2):
                            cmp_1b(128, ncv, t, g)
                        for g in range(2):
                            tiles = []
                            for kt in range(max(0, t - 4), t + 1):
                                Fb = (F0, B_F0) if kt == t else ((F1, B_F1) if kt == t - 1 else ((FW, B_FW) if kt == t - 4 else None))
                                rg = kt % 8
                                tiles.append((KTwin[:, rg, :], Vwin[:, rg, g, :], 128, Fb, False, [B_KTwin, B_Vwin]))
                            attend(128, qsl, tiles, None, g, [True])
                            combine(128, g, 2, False)
                        for g in range(2):
                            cmp_part2(128, qsl, ncv, g)
                        for g in range(2):
                            tiles = []
                            for kt in range(t + 1):
                                Fb = (F0, B_F0) if kt == t else ((F1, B_F1) if kt == t - 1 else None)
                                tiles.append((KE[g][0][:, kt * 128:(kt + 1) * 128], Vsel[:, kt, g, :], 128, Fb, True, [KE[g][1], B_Vsel]))
                            attend(128, qsl, tiles, None, g, [True])
                            combine(128, g, 1, False)
                    else:
                        kb.op("pool", lambda e: e.memset(o_att[:], 0.0), w=[B_oatt])
                        if nxt is not None:
                            prep_T(1 - par)
                    prev_pool = [] if t == 0 else [(u_bf[1 - par][0], 1, u_bf[1 - par][1])]
                    finish_tile(128, xtt, B_xt, par, u_bf[par], prev_pool, (2 if t == 0 else 0,), xscr_p[t * 128:(t + 1) * 128, :], B_xscr_p[t])
                if do_sample:
                    par = NT % 2
                    fk, fkb, zc = project(par, 2048)
                    xtt, B_xt = xt[par]
                    kb.op("act", lambda e: e.copy(out=KnT[:, :], in_=fk[:, 256:384]), r=[fkb], w=[B_KnT])
                    evac_tokmajor(zc, u_bf[par])
                    kb.op("pool", lambda e: e.memset(Vn[:, :, :, 64:65], 1.0), w=[B_Vn])
                    kb.op("pool", lambda e: e.tensor_copy(out=Vn[:, 0, :, 0:64], in_=kvrow[:, 384:512].rearrange("p (g d) -> p g d", g=2)), r=[B_kv], w=[B_Vn])
                    kb.op("pool", lambda e: e.tensor_copy(out=Vn[:, 1, :, 0:64], in_=kvrow[:, 640:768].rearrange("p (g d) -> p g d", g=2)), r=[B_kv], w=[B_Vn])
                    kb.dma("sp", ncmp_s[l, :, :], kvrow[:, 0:256], r=[B_kv], final=True)
                    kb.dma("sp", nsel_s[l, :, :], kvrow[:, 256:512], r=[B_kv], final=True)
                    kb.dma("sp", nwin_s[l, :, 0:504, :], swin[l, :, 8:512, :], final=True)
                    kb.dma("sp", npool_s[l, :, 0:7, :], spool[l, :, 8:15, :], final=True)
                    for b in range(16):
                        kb.dma("sp", nwin_s[l, b, 504:512, :], kvrow[b * 8:(b + 1) * 8, 512:768], r=[B_kv], final=True)
                        kb.dma("sp", npool_s[l, b, 7:15, :], u_f[b * 8:(b + 1) * 8, :], r=[B_uf], final=True)
                    for b in range(16):
                        kb.dma("sp", gates_s[0:8, b, :], gates[b * 8:(b + 1) * 8, :], r=[B_gates], w=[B_gates_s])
                    kb.op("pool", lambda e: e.memset(sp_bf[:], 0.0), w=[B_spbf])
                    for hf in range(2):
                        kb.dma("pool", sp_bf[0:120, hf, :], spool[l, hf * 8:(hf + 1) * 8, :, :].rearrange("b r c -> (b r) c"), w=[B_spbf])
                    kb.op("pool", lambda e: e.memset(Psum_g[:], 0.0), w=[B_Psg])
                    kb.op("pool", lambda e: e.memset(compKT[:], 0.0), w=[B_cKT])
                    kb.op("pool", lambda e: e.memset(compVT[:], 0.0), w=[B_cVT])
                    if do_attn:
                        grot = [0]
                        for b in range(16):
                            qsl = slice(b * 8, b * 8 + 8)
                            for ci, cache in enumerate((ccmp, csel)):
                                for half in range(2):
                                    grot[0] += 1
                                    gbt, gbb = Gb[grot[0] % 2]
                                    for j in range(8):
                                        pg = half * 8 + j
                                        kb.op("pool", lambda e, gbt=gbt, j=j, pg=pg, cache=cache: e.indirect_dma_start(
                                            out=gbt[:, j, :], out_offset=None, in_=cache.ap().rearrange("l r c -> (l r) c"),
                                            in_offset=bass.IndirectOffsetOnAxis(ap=IDX[:, b * 16 + pg:b * 16 + pg + 1], axis=0),
                                            element_offset=l * NPHYS * 128 * 256),
                                            r=[B_IDX], w=[gbb], dma=True)
                                    c0 = half * 1024
                                    if ci == 0:
                                        tbt, tbb = next_tb()
                                        for j in range(8):
                                            kb.op("pe", lambda e, tbt=tbt, gbt=gbt, j=j: e.transpose(out=tbt[:, j * 128:(j + 1) * 128], in_=gbt[:, j, 0:128], identity=ident[:]), r=[gbb, B_ident], w=[tbb])
                                        kb.op("act", lambda e, tbt=tbt, c0=c0: e.copy(out=KsT[:, c0:c0 + 1024], in_=tbt[:, :]), r=[tbb], w=[B_KsT])
                                        tbt2, tbb2 = next_tb()
                                        for j in range(8):
                                            kb.op("pe", lambda e, tbt2=tbt2, gbt=gbt, j=j: e.transpose(out=tbt2[:, j * 128:(j + 1) * 128], in_=gbt[:, j, 128:256], identity=ident[:]), r=[gbb, B_ident], w=[tbb2])
                                        kb.op("dve", lambda e, tbt2=tbt2, c0=c0: e.tensor_copy(out=VsT[:, c0:c0 + 1024], in_=tbt2[:, :]), r=[tbb2], w=[B_VsT])
                                    else:
                                        for g in range(2):
                                            tbt, tbb = next_tb()
                                            for j in range(8):
                                                kb.op("pe", lambda e: e.transpose(out=tbt[0:64, j * 128:(j + 1) * 128], in_=gbt[:, j, 64 * g:64 * g + 64], identity=ident[:]), r=[gbb, B_ident], w=[tbb])
                                            if g == 0:
                                                kb.op("act", lambda e: e.copy(out=KE[g][0][0:64, c0:c0 + 1024], in_=tbt[0:64, :]), r=[tbb], w=[KE[g][1]])
                                            else:
                                                kb.op("dve", lambda e: e.tensor_copy(out=KE[g][0][0:64, c0:c0 + 1024], in_=tbt[0:64, :]), r=[tbb], w=[KE[g][1]])
                                        kb.op("pool", lambda e, gbt=gbt, half=half: e.tensor_copy(out=Vsel[:, half * 8:half * 8 + 8, :, 0:64], in_=gbt[:, :, 128:256].rearrange("p j (g d) -> p j g d", g=2)), r=[gbb], w=[B_Vsel])
                            grot[0] += 1
                            gbt, gbb = Gb[grot[0] % 2]
                            kb.dma("pool", gbt[:, 0:4, :], swin[l, b, :, :].rearrange("(j p) c -> p j c", p=128), w=[gbb])
                            tbt, tbb = next_tb()
                            for j in range(4):
                                kb.op("pe", lambda e, tbt=tbt, gbt=gbt, j=j: e.transpose(out=tbt[:, j * 128:(j + 1) * 128], in_=gbt[:, j, 0:128], identity=ident[:]), r=[gbb, B_ident], w=[tbb])
                            kb.op("act", lambda e, tbt=tbt: e.copy(out=KTwin[:, 0:4, :].rearrange("p a b -> p (a b)"), in_=tbt[:, 0:512]), r=[tbb], w=[B_KTwin])
                            kb.op("pool", lambda e, gbt=gbt: e.tensor_copy(out=Vwin[:, 0:4, :, 0:64], in_=gbt[:, 0:4, 128:256].rearrange("p j (g d) -> p j g d", g=2)), r=[gbb], w=[B_Vwin])
                            for (srcT, Bs, soff, wbd, Bw, dstT, Bd, j) in ((KsT, B_KsT, 0, wk_bd, B_wk, compKT, B_cKT, 0), (VsT, B_VsT, 0, wv_bd, B_wv, compVT, B_cVT, 1)):
                                wbt, wbb = next_wb()
                                for ll in range(32):
                                    kb.op("pe", lambda e, wbt=wbt, wbd=wbd, srcT=srcT, soff=soff, ll=ll: e.matmul(wbt[:, 0:127], lhsT=wbd[:, ll, :], rhs=srcT[:, soff + ll:soff + ll + 16 * 126 + 1:16], start=(ll == 0), stop=(ll == 31)),
                                          r=[Bw, Bs], w=[wbb])
                                kb.op("act", lambda e, wbt=wbt, dstT=dstT, j=j: e.activation(out=dstT[:, 0:127], in_=wbt[:, 0:127], func=AF.Identity, bias=biasKV[:, j:j + 1]), r=[wbb, B_bkv], w=[Bd])
                            tbt, tbb = next_tb()
                            kb.op("pe", lambda e, tbt=tbt: e.transpose(out=tbt[:, 0:128], in_=compVT[:, 0:128], identity=ident[:]), r=[B_cVT, B_ident], w=[tbb])
                            kb.op("dve", lambda e, tbt=tbt: e.tensor_copy(out=compV[:, 0, :], in_=tbt[:, 0:128]), r=[tbb], w=[B_cV])
                            gs_b = (gates_s[0:8, b, :], B_gates_s)
                            kb.op("pool", lambda e: e.memset(o_att[0:32, :], 0.0), w=[B_oatt])
                            for g in range(2):
                                cmp_1a(8, qsl, 128, 0, 9, g)
                            for g in range(2):
                                cmp_1b(8, 128, 16, g)
                            fsb_b = FSB[:, b, :].rearrange("p (h q) -> p h q", h=8)
                            for g in range(2):
                                tiles = []
                                for kt in range(4):
                                    Fb = (FW, B_FW) if kt == 0 else ((F1, B_F1) if kt == 3 else None)
                                    tiles.append((KTwin[:, kt, :], Vwin[:, kt, g, :], 128, Fb, False, [B_KTwin, B_Vwin]))
                                tiles.append((KnT[:, :], Vn[:, 1, g, :], 128, (fsb_b, B_FSB), False, [B_KnT, B_Vn]))
                                attend(8, qsl, tiles, None, g, [True], fsl=slice(0, 8))
                                combine(8, g, 2, False, gsrc=gs_b)
                            for g in range(2):
                                cmp_part2(8, qsl, 128, g, gsrc=gs_b)
                            for g in range(2):
                                tiles = []
                                for kt in range(16):
                                    Fb = (F1, B_F1) if kt == 15 else None
                                    tiles.append((KE[g][0][:, kt * 128:(kt + 1) * 128], Vsel[:, kt, g, :], 128, Fb, True, [KE[g][1], B_Vsel]))
                                tiles.append((KE[g][0][:, 2048:2176], Vn[:, 0, g, :], 128, (fsb_b, B_FSB), True, [KE[g][1], B_Vn]))
                                attend(8, qsl, tiles, None, g, [True], fsl=slice(0, 8))
                                combine(8, g, 1, False, gsrc=gs_b)
                            kb.dma("sp", o_att_s[b * 8:(b + 1) * 8, :], o_att[0:8, :], r=[B_oatt], w=[B_oatts])
                    else:
                        kb.op("pool", lambda e: e.memset(o_att_s[:], 0.0), w=[B_oatts])
                    prev_pool = [(sp_bf[:, 0, :], 4, B_spbf), (sp_bf[:, 1, :], 5, B_spbf)]
                    finish_tile(128, xtt, B_xt, par, u_bf[par], prev_pool, (3,), xscr_s[:, :], B_xscr_s, oa=(o_att_s, B_oatts))
                kb.barrier()

            if do_ffn:
                with contextlib.ExitStack() as stC:
                    def sc(name, shape, dt):
                        return stC.enter_context(nc.sbuf_tensor(un(name), list(shape), dt)), Buf(name)
                    wo_sb, B_wo = sc("wo_sb", [128, 22, D], BF16)
                    gffn, B_gffn = sc("gffn", [128, D], F32)
                    gfin, B_gfin = sc("gfin", [128, D], F32)
                    wg = [sc("wg%d" % i, [128, 8, 512], BF16) for i in range(2)]
                    wu = [sc("wu%d" % i, [128, 8, 512], BF16) for i in range(2)]
                    xc = [sc("xc%d" % i, [128, D], F32) for i in range(4)]
                    hTc, B_hTc = sc("hTc", [128, 8, 512], BF16)
                    actT, B_actT = sc("actT", [128, 22, 512], BF16)
                    junk2, B_junk2 = sc("junk2", [128, D], BF16)
                    ss2, B_ss2 = sc("ss2", [128, 2], F32)
                    h2, B_h2 = sc("h2", [128, D], BF16)
                    sg = [sc("sg%d" % i, [128, 512], F32) for i in range(2)]
                    xo = [sc("xo%d" % i, [128, D], F32) for i in range(2)]
                    yo = [sc("yo%d" % i, [128, D], F32) for i in range(2)]
                    for j in range(22):
                        kb.dma("pool", wo_sb[:, j, :], w_ffn_out[l, j * 128:(j + 1) * 128, :], w=[B_wo])
                    kb.dma("sp", gffn[:], bass.AP(tensor=norm_ffn, offset=l * D, ap=[[0, 128], [1, D]]), w=[B_gffn])
                    kb.dma("sp", gfin[:], bass.AP(tensor=norm_final, offset=0, ap=[[0, 128], [1, D]]), w=[B_gfin])
                    chunks = [("p", c * 4, min(4, NT - c * 4)) for c in range((NT + 3) // 4)]
                    if do_sample:
                        chunks.append(("s", 0, 1))
                    wrot = [0]
                    orot = [0]
                    for (kind, t0, ntile) in chunks:
                        ntok = ntile * 128
                        for i in range(ntile):
                            xct, B_xc = xc[i]
                            if kind == "p":
                                kb.dma("sp", xct[:], xscr_p[(t0 + i) * 128:(t0 + i + 1) * 128, :], r=[B_xscr_p[t0 + i]], w=[B_xc])
                            else:
                                kb.dma("sp", xct[:], xscr_s[:, :], r=[B_xscr_s], w=[B_xc])
                            kb.op("act", lambda e, xct=xct: e.activation(out=junk2[:], in_=xct[:], func=AF.Square, accum_out=ss2[:, 0:1]), r=[B_xc], w=[B_junk2, B_ss2])
                            kb.op("dve", lambda e: e.tensor_scalar(out=ss2[:, 1:2], in0=ss2[:, 0:1], scalar1=1.0 / D, scalar2=EPS, op0=ALU.mult, op1=ALU.add), r=[B_ss2], w=[B_ss2])
                            kb.op("act", lambda e: e.activation(out=ss2[:, 1:2], in_=ss2[:, 1:2], func=AF.Sqrt), r=[B_ss2], w=[B_ss2])
                            kb.op("dve", lambda e: e.reciprocal(out=ss2[:, 1:2], in_=ss2[:, 1:2]), r=[B_ss2], w=[B_ss2])
                            kb.op("dve", lambda e, xct=xct: e.scalar_tensor_tensor(out=h2[:], in0=xct[:], scalar=ss2[:, 1:2], in1=gffn[:], op0=ALU.mult, op1=ALU.mult), r=[B_xc, B_ss2, B_gffn], w=[B_h2])
                            tbt, tbb = next_tb()
                            for k in range(8):
                                kb.op("pe", lambda e, tbt=tbt, k=k: e.transpose(out=tbt[:, k * 128:(k + 1) * 128], in_=h2[:, k * 128:(k + 1) * 128], identity=ident[:]), r=[B_h2, B_ident], w=[tbb])
                            kb.op("act", lambda e, tbt=tbt, i=i: e.copy(out=hTc[:, :, i * 128:(i + 1) * 128], in_=tbt[:, :].rearrange("p (a b) -> p a b", a=8)), r=[tbb], w=[B_hTc])
                        for jg in range(6):
                            nj = 4 if jg < 5 else 2
                            wrot[0] += 1
                            wgt, B_wg = wg[wrot[0] % 2]
                            wut, B_wu = wu[wrot[0] % 2]
                            for k in range(8):
                                kb.dma("pool", wgt[:, k, 0:nj * 128], w_ffn_in[l, k * 128:(k + 1) * 128, jg * 512:jg * 512 + nj * 128], w=[B_wg])
                                kb.dma("pool", wut[:, k, 0:nj * 128], w_ffn_in[l, k * 128:(k + 1) * 128, DFF + jg * 512:DFF + jg * 512 + nj * 128], w=[B_wu])
                            for jj in range(nj):
                                j = jg * 4 + jj
                                gt_, gb_ = next_wb()
                                for k in range(8):
                                    kb.op("pe", lambda e, gt_=gt_, wgt=wgt, jj=jj, k=k: e.matmul(gt_[:, 0:ntok], lhsT=wgt[:, k, jj * 128:(jj + 1) * 128], rhs=hTc[:, k, 0:ntok], start=(k == 0), stop=(k == 7)), r=[B_wg, B_hTc], w=[gb_])
                                ut_, ub_ = next_wb()
                                for k in range(8):
                                    kb.op("pe", lambda e, ut_=ut_, wut=wut, jj=jj, k=k: e.matmul(ut_[:, 0:ntok], lhsT=wut[:, k, jj * 128:(jj + 1) * 128], rhs=hTc[:, k, 0:ntok], start=(k == 0), stop=(k == 7)), r=[B_wu, B_hTc], w=[ub_])
                                sgt, sgb = sg[j % 2]
                                kb.op("act", lambda e, gt_=gt_, sgt=sgt: e.activation(out=sgt[:, 0:ntok], in_=gt_[:, 0:ntok], func=AF.Silu), r=[gb_], w=[sgb])
                                kb.op("dve", lambda e, ut_=ut_, sgt=sgt, j=j: e.tensor_tensor(out=actT[:, j, 0:ntok], in0=ut_[:, 0:ntok], in1=sgt[:, 0:ntok], op=ALU.mult), r=[ub_, sgb], w=[B_actT])
                        for i in range(ntile):
                            xct, B_xc = xc[i]
                            orot[0] += 1
                            xot, B_xo = xo[orot[0] % 2]
                            for hf in range(2):
                                wo_, wob_ = next_wb()
                                for j in range(22):
                                    kb.op("pe", lambda e, wo_=wo_, j=j, i=i, hf=hf: e.matmul(wo_[:, :], lhsT=actT[:, j, i * 128:(i + 1) * 128], rhs=wo_sb[:, j, hf * 512:(hf + 1) * 512], start=(j == 0), stop=(j == 21)), r=[B_actT, B_wo], w=[wob_])
                                kb.op("dve", lambda e, wo_=wo_, xot=xot, xct=xct, hf=hf: e.tensor_tensor(out=xot[:, hf * 512:(hf + 1) * 512], in0=wo_[:, :], in1=xct[:, hf * 512:(hf + 1) * 512], op=ALU.add), r=[wob_, B_xc], w=[B_xo])
                            if not last:
                                if kind == "p":
                                    kb.dma("sp", xscr_p[(t0 + i) * 128:(t0 + i + 1) * 128, :], xot[:], r=[B_xo], w=[B_xscr_p[t0 + i]])
                                else:
                                    kb.dma("sp", xscr_s[:, :], xot[:], r=[B_xo], w=[B_xscr_s])
                            else:
                                yot, B_yo = yo[orot[0] % 2]
                                kb.op("act", lambda e, xot=xot: e.activation(out=junk2[:], in_=xot[:], func=AF.Square, accum_out=ss2[:, 0:1]), r=[B_xo], w=[B_junk2, B_ss2])
                                kb.op("dve", lambda e: e.tensor_scalar(out=ss2[:, 1:2], in0=ss2[:, 0:1], scalar1=1.0 / D, scalar2=EPS, op0=ALU.mult, op1=ALU.add), r=[B_ss2], w=[B_ss2])
                                kb.op("act", lambda e: e.activation(out=ss2[:, 1:2], in_=ss2[:, 1:2], func=AF.Sqrt), r=[B_ss2], w=[B_ss2])
                                kb.op("dve", lambda e: e.reciprocal(out=ss2[:, 1:2], in_=ss2[:, 1:2]), r=[B_ss2], w=[B_ss2])
                                kb.op("dve", lambda e, xot=xot, yot=yot: e.scalar_tensor_tensor(out=yot[:], in0=xot[:], scalar=ss2[:, 1:2], in1=gfin[:], op0=ALU.mult, op1=ALU.mult), r=[B_xo, B_ss2, B_gfin], w=[B_yo])
                                if kind == "p":
                                    kb.dma("sp", y_p[(t0 + i) * 128:(t0 + i + 1) * 128, :], yot[:], r=[B_yo], final=True)
                                else:
                                    kb.dma("sp", y_s[:, :], yot[:], r=[B_yo], final=True)
                    kb.barrier()
        kb.emit()
    return nc


_LAST_NC = [None]


def build_safe(*a, **k):
    try:
        return build(*a, **k)
    except StopIteration:
        return _LAST_NC[0]


_CONST_KEYS = ("ident", "j32", "oh", "negrow", "ex", "fc", "pb", "selb", "neg64", "iota_p")


def make_in_map(inp, consts, L, pb_idx, sb0):
    f = lambda a: np.ascontiguousarray(np.asarray(a))
    nphys = inp["cache_cmp"].shape[1]
    m = {
        "xp": f(inp["x_prompt"][pb_idx]),
        "xs": f(np.asarray(inp["x_sample"])[sb0:sb0 + 16].reshape(128, D)),
        "ccmp": f(np.asarray(inp["cache_cmp"]).reshape(L, nphys * 128, 256)),
        "csel": f(np.asarray(inp["cache_sel"]).reshape(L, nphys * 128, 256)),
        "swin": f(np.asarray(inp["state_win"])[:, sb0:sb0 + 16].reshape(L, 16, 512, 256)),
        "spool": f(np.asarray(inp["state_pool"])[:, sb0:sb0 + 16]),
        "ptab": f(np.asarray(inp["page_table"])[sb0:sb0 + 16].astype(np.int32)),
        "rel_bias": f(inp["rel_bias"]),
        "norm_mix": f(inp["norm_mix"]),
        "norm_ffn": f(inp["norm_ffn"]),
        "norm_final": f(np.asarray(inp["norm_final"]).reshape(1, D)),
        "w_in": f(inp["w_in"]),
        "w_out": f(inp["w_out"]),
        "cmp_pos": f(inp["cmp_pos"]),
        "w_cmp": f(inp["w_cmp"]),
        "w_pool": f(inp["w_pool"]),
        "pool_scale": f(inp["pool_scale"]),
        "w_ffn_in": f(inp["w_ffn_in"]),
        "w_ffn_out": f(inp["w_ffn_out"]),
    }
    for k in _CONST_KEYS:
        m[k] = consts[k]
    return m


def kernel(**inputs):
    B, T, _ = inputs["x_prompt"].shape
    L = inputs["w_in"].shape[0]
    nphys = inputs["cache_cmp"].shape[1]
    nc = build(T, L, 16, nphys, do_sample=DO_SAMPLE)
    consts = host_consts(max(T, 2304))
    in_maps = [make_in_map(inputs, consts, L, c // 2, 16 * c) for c in range(NCORES)]
    res = run_bass_kernel_spmd(nc, in_maps, core_ids=list(range(NCORES)))
    rs = res.results
    y_p = np.stack([rs[2 * b]["y_p"] for b in range(B)])
    y_s = np.concatenate([rs[c]["y_s"].reshape(16, 8, D) for c in range(NCORES)], axis=0)

    def pk(name, shape_tail):
        return np.stack([rs[2 * b][name] for b in range(B)], axis=1).reshape((L, B) + shape_tail)

    def sk(name, per, shape_tail):
        return np.concatenate([rs[c][name].reshape((L, 16) + per) for c in range(NCORES)], axis=1).reshape((L, 128) + shape_tail)
    return (y_p.astype(np.float32), y_s.astype(np.float32),
            pk("ncmp_p", (T, 2, 2, 64)), pk("nsel_p", (T, 2, 2, 64)), pk("nwin_p", (512, 2, 2, 64)), pk("npool_p", (15, 512)),
            sk("ncmp_s", (8, 256), (8, 2, 2, 64)), sk("nsel_s", (8, 256), (8, 2, 2, 64)),
            sk("nwin_s", (512, 256), (512, 2, 2, 64)), sk("npool_s", (15, 512), (15, 512)))
```

```python
import contextlib
import numpy as np
import ml_dtypes
import concourse.bass as bass
import concourse.mybir as mybir
from concourse.bass_utils import run_bass_kernel_spmd

F32 = mybir.dt.float32
BF16 = mybir.dt.bfloat16
I32 = mybir.dt.int32
AF = mybir.ActivationFunctionType
ALU = mybir.AluOpType

D = 1024
INC = 1816
DFF = 2816
NEG = -30000.0
EPS = 1e-6
NSEQ = 1152
NCORES = 8
DO_SAMPLE = True


class Buf:
    __slots__ = ("name", "last_w", "readers", "excl")

    def __init__(self, name, excl=False):
        self.name = name
        self.excl = excl
        self.last_w = None
        self.readers = {}


class _Rec:
    def __getattr__(self, name):
        def f(*a, **k):
            self.call = (name, a, k)
            return self
        return f


class KB:
    ENG = ("sp", "act", "pe", "dve", "pool")
    NDMA = {"sp": 8, "act": 2, "pool": 6}

    def __init__(self, nc):
        self.nc = nc
        self.prog = {e: [] for e in self.ENG}
        self.cnt = {e: 0 for e in self.ENG}
        self.waited = {e: {} for e in self.ENG}
        self.pending = {e: {} for e in self.ENG}
        self.dma_i = {e: 0 for e in self.NDMA}
        self.dma_val = {}
        self.final_tokens = []

    def _deps(self, eng, r, w):
        deps = dict(self.pending[eng])
        self.pending[eng] = {}

        def add(tok):
            if tok is None:
                return
            k, v = tok
            if deps.get(k, 0) < v:
                deps[k] = v
        for b in r:
            add(b.last_w)
        for b in w:
            add(b.last_w)
            for k, v in b.readers.items():
                if k == eng:
                    continue
                add((k, v))
        out = []
        wd = self.waited[eng]
        for k, v in deps.items():
            if k == "pe" and eng == "pe":
                continue
            if wd.get(k, 0) >= v:
                continue
            wd[k] = v
            out.append((k, v))
        return out

    def op(self, eng, fn, r=(), w=(), dma=False, final=False):
        if not isinstance(fn, tuple):
            rec = _Rec()
            fn(rec)
            fn = rec.call
        ex = [b for b in r if b.excl]
        if ex:
            w = list(w) + [b for b in ex if b not in w]
            r = [b for b in r if not b.excl]
        waits = self._deps(eng, r, w)
        if dma:
            slot = self.dma_i[eng] % self.NDMA[eng]
            self.dma_i[eng] += 1
            key = ("dma", eng, slot)
            val = self.dma_val.get(key, 0) + 16
            self.dma_val[key] = val
            tok = (key, val)
            inc = (key, 16)
        else:
            self.cnt[eng] += 1
            tok = (eng, self.cnt[eng])
            inc = (eng, 1)
        self.prog[eng].append((waits, fn, inc))
        k, v = tok
        for b in r:
            if b.readers.get(k, 0) < v:
                b.readers[k] = v
        for b in w:
            b.last_w = tok
            b.readers = {}
        if final:
            self.final_tokens.append(tok)
        return tok

    def dma(self, eng, out, in_, r=(), w=(), final=False, **kw):
        return self.op(eng, ("dma_start", (), dict(out=out, in_=in_, **kw)), r=r, w=w, dma=True, final=final)

    def barrier(self):
        allv = {e: self.cnt[e] for e in self.ENG if self.cnt[e] > 0}
        allv.update(self.dma_val)
        for e in self.ENG:
            for k, v in allv.items():
                if k == e:
                    continue
                if self.pending[e].get(k, 0) < v:
                    self.pending[e][k] = v

    def emit(self):
        nc = self.nc
        keys = list(self.ENG) + [("dma", e, s) for e in self.NDMA for s in range(self.NDMA[e])]
        with contextlib.ExitStack() as st:
            sems = {}
            for k in keys:
                nm = k if isinstance(k, str) else "d_%s_%d" % (k[1], k[2])
                sems[k] = st.enter_context(nc.semaphore("s_" + nm))
            block = st.enter_context(nc.Block())
            fin = {}
            for k, v in self.final_tokens:
                if fin.get(k, 0) < v:
                    fin[k] = v

            def run(ename, e):
                for waits, fn, inc in self.prog[ename]:
                    for k, v in waits:
                        e.wait_ge(sems[k], v)
                    ins = getattr(e, fn[0])(*fn[1], **fn[2])
                    ins.then_inc(sems[inc[0]], inc[1])
                if ename == "sp":
                    for k, v in fin.items():
                        e.wait_ge(sems[k], v)

            @block.sync
            def _(e):
                run("sp", e)

            @block.scalar
            def _(e):
                run("act", e)

            @block.tensor
            def _(e):
                run("pe", e)

            @block.vector
            def _(e):
                run("dve", e)

            @block.gpsimd
            def _(e):
                run("pool", e)


def _bucket(n):
    n = np.maximum(n, 0)
    nf = np.maximum(n, 1).astype(np.float32)
    large = 16 + (np.log(nf / np.float32(16)) / np.float32(np.log(8.0)) * np.float32(16)).astype(np.int32)
    return np.where(n < 16, n, np.minimum(large, 31))


def host_consts(TE):
    c = {}
    c["ident"] = np.eye(128, dtype=np.float32).astype(ml_dtypes.bfloat16)
    c["j32"] = np.eye(128, dtype=np.float32)[::-1].copy()
    oh = np.zeros((33, NSEQ), np.float32)

    def put(col, n):
        if n < 0:
            oh[32, col] = 1.0
        else:
            oh[int(_bucket(np.array([n]))[0]), col] += 1.0
            oh[31, col] -= 1.0
    for n in range(384):
        put(n, n - 127)
    for n in range(256):
        if n >= 127:
            oh[32, 384 + n] = 1.0
    for s_i, s in enumerate((0, 128)):
        for m in range(256):
            put(640 + 256 * s_i + m, 127 + s - m)
    c["oh"] = oh
    c["negrow"] = np.full((1, 8), NEG, np.float32)
    ex = np.zeros((64, TE), np.float32)
    for j in range(64):
        ex[j, j * 64:(j + 1) * 64] = 1.0
    c["ex"] = ex.astype(ml_dtypes.bfloat16)
    fc = np.zeros((128, 128), np.float32)
    for q in range(128):
        cur = 1 if q >= 64 else 0
        for jj in range(128):
            jr = jj - 64
            if jr == cur or jr == cur - 1:
                fc[q, jj] = 1e9
            elif jr > cur:
                fc[q, jj] = -1e9
    c["fc"] = fc
    pb = np.zeros((6, 4, 128, 128), np.float32)
    for k, w in enumerate((2, 4, 8, 16)):
        for t in range(128):
            for tp in range(128):
                if 0 <= t - tp < w:
                    pb[0, k, tp, t] += 1.0 / w
                    pb[2, k, tp, t] += 1.0 / min(t + 1, w)
                    if tp // 8 == t // 8:
                        pb[3, k, tp, t] += 1.0 / w
                if t + 128 - tp < w:
                    pb[1, k, tp, t] += 1.0 / w
            pb[0, k, t, t] -= 1.0
            pb[2, k, t, t] -= 1.0
            pb[3, k, t, t] -= 1.0
        for hf in range(2):
            for rp in range(120):
                b = hf * 8 + rp // 15
                r = rp % 15
                for qi in range(8):
                    if (15 - r) + qi < w:
                        pb[4 + hf, k, rp, b * 8 + qi] = 1.0 / w
    c["pb"] = pb.astype(ml_dtypes.bfloat16)
    selb = np.zeros((9, 16, 128), np.float32)
    for b in range(16):
        for kp in range(8):
            selb[kp, b, b * 8 + kp] = 1.0
        for k in range(128):
            if k // 8 != b:
                selb[8, b, k] = 1.0
    c["selb"] = selb
    c["neg64"] = np.full((1, 64), NEG, np.float32)
    c["iota_p"] = np.arange(128, dtype=np.float32).reshape(128, 1)
    return c


def build(T, L, NSEL, NPHYS, do_sample=True, do_ffn=True, do_attn=True, stage=99):
    NT = T // 128
    TE = max(T, 2304)
    nc = bass.Bass("TRN2", target_bir_lowering=False)
    _LAST_NC[0] = nc

    def din(name, shape, dt=F32):
        return nc.dram_tensor(name, list(shape), dt, kind="ExternalInput")

    def dout(name, shape, dt=F32):
        return nc.dram_tensor(name, list(shape), dt, kind="ExternalOutput")

    def dscr(name, shape, dt=F32):
        return nc.dram_tensor(name, list(shape), dt, kind="Internal")

    xp = din("xp", [T, D])
    xs = din("xs", [128, D])
    ccmp = din("ccmp", [L, NPHYS * 128, 256])
    csel = din("csel", [L, NPHYS * 128, 256])
    swin = din("swin", [L, 16, 512, 256])
    spool = din("spool", [L, 16, 15, 512])
    ptab = din("ptab", [16, 16], I32)
    rel_bias = din("rel_bias", [32, 8])
    norm_mix = din("norm_mix", [L, D])
    norm_ffn = din("norm_ffn", [L, D])
    norm_final = din("norm_final", [1, D])
    w_in = din("w_in", [L, D, INC])
    w_out = din("w_out", [L, D, D])
    cmp_pos = din("cmp_pos", [L, 32, 2, 64])
    w_cmp = din("w_cmp", [L, 2, 32, 64, 64])
    w_pool = din("w_pool", [L, 4, 128, 128])
    pool_scale = din("pool_scale", [L, 512])
    w_ffn_in = din("w_ffn_in", [L, D, 2 * DFF])
    w_ffn_out = din("w_ffn_out", [L, DFF, D])
    c_ident = din("ident", [128, 128], BF16)
    c_j32 = din("j32", [128, 128])
    c_oh = din("oh", [33, NSEQ])
    c_negrow = din("negrow", [1, 8])
    c_ex = din("ex", [64, TE], BF16)
    c_fc = din("fc", [128, 128])
    c_pb = din("pb", [6, 4, 128, 128], BF16)
    c_selb = din("selb", [9, 16, 128])
    c_neg64 = din("neg64", [1, 64])
    c_iota = din("iota_p", [128, 1])

    y_p = dout("y_p", [T, D])
    y_s = dout("y_s", [128, D])
    ncmp_p = dout("ncmp_p", [L, T, 256])
    nsel_p = dout("nsel_p", [L, T, 256])
    nwin_p = dout("nwin_p", [L, 512, 256])
    npool_p = dout("npool_p", [L, 15, 512])
    ncmp_s = dout("ncmp_s", [L, 128, 256])
    nsel_s = dout("nsel_s", [L, 128, 256])
    nwin_s = dout("nwin_s", [L, 16, 512, 256])
    npool_s = dout("npool_s", [L, 16, 15, 512])

    xscr_p = dscr("xscr_p", [T, D])
    xscr_s = dscr("xscr_s", [128, D])
    fvd = dscr("fvd", [8, NSEQ])

    kb = KB(nc)
    B_xscr_p = [Buf("xscr_p%d" % i) for i in range(NT)]
    B_xscr_s = Buf("xscr_s")
    B_fvd = Buf("fvd")

    def bc_ap(ap, shape_tail):
        a = [list(x) for x in ap.ap]
        for s in shape_tail:
            a.append([0, s])
        return bass.AP(tensor=ap.tensor, offset=ap.offset, ap=a)

    with contextlib.ExitStack() as st:
        uid = [0]

        def un(name):
            uid[0] += 1
            return "t%d_%s" % (uid[0], name)

        def sb(name, shape, dt):
            return st.enter_context(nc.sbuf_tensor(un(name), list(shape), dt)), Buf(name)

        def psb(name, shape, dt):
            return st.enter_context(nc.psum_tensor(un(name), list(shape), dt)), Buf(name, excl=True)

        TB = [psb("tb%d" % i, [128, 1024], BF16) for i in range(2)]
        WB = [psb("wb%d" % i, [128, 512], F32) for i in range(4)]
        OB = [psb("ob%d" % i, [128, 512], F32) for i in range(2)]
        rot = {"tb": 0, "wb": 0}

        def next_tb():
            rot["tb"] += 1
            return TB[rot["tb"] % 2]

        def next_wb():
            rot["wb"] += 1
            return WB[rot["wb"] % 4]

        ident, B_ident = sb("ident", [128, 128], BF16)
        F0, B_F0 = sb("F0", [128, 8, 128], F32)
        F1, B_F1 = sb("F1", [128, 8, 128], F32)
        FW, B_FW = sb("FW", [128, 8, 128], F32)
        GN, B_GN = sb("GN", [128, 8, 16], F32)
        FC, B_FC = sb("FC", [128, 128], F32)
        PB, B_PB = sb("PB", [128, 24, 128], BF16)
        big9, B_big9 = sb("big9", [128, 1], F32)
        IDX, B_IDX = sb("IDX", [128, 256], I32)
        FSB, B_FSB = sb("FSB", [128, 16, 64], F32)

        kb.dma("sp", ident[:], c_ident.ap(), w=[B_ident])
        kb.dma("sp", FC[:], c_fc.ap(), w=[B_FC])
        kb.dma("sp", PB[:].rearrange("p (a b) t -> p a b t", a=6), c_pb.ap().rearrange("a b p t -> p a b t"), w=[B_PB])
        kb.op("pool", lambda e: e.memset(big9[:], 1e9), w=[B_big9])

        with contextlib.ExitStack() as st2:
            def sb2(name, shape, dt):
                return st2.enter_context(nc.sbuf_tensor(un(name), list(shape), dt)), Buf(name)
            tbl, B_tbl = sb2("tbl", [33, 8], F32)
            ohs, B_ohs = sb2("ohs", [33, NSEQ], F32)
            fvs, B_fvs = sb2("fvs", [8, NSEQ], F32)
            j32, B_j32 = sb2("j32", [128, 128], F32)
            hk, B_hk = sb2("hk", [128, 8, 128], F32)
            T0, B_T0 = sb2("T0", [128, 8, 128], F32)
            T1, B_T1 = sb2("T1", [128, 8, 128], F32)
            kb.dma("sp", tbl[0:32, :], rel_bias.ap(), w=[B_tbl])
            kb.dma("sp", tbl[32:33, :], c_negrow.ap(), w=[B_tbl])
            kb.dma("sp", ohs[:], c_oh.ap(), w=[B_ohs])
            kb.dma("sp", j32[:], c_j32.ap(), w=[B_j32])
            for ci, (c0, c1) in enumerate(((0, 512), (512, 1024), (1024, NSEQ))):
                wbt, wbb = next_wb()
                kb.op("pe", lambda e, wbt=wbt, c0=c0, c1=c1: e.matmul(wbt[0:8, 0:c1 - c0], lhsT=tbl[:, :], rhs=ohs[:, c0:c1], start=True, stop=True),
                      r=[B_tbl, B_ohs], w=[wbb])
                kb.op("dve", lambda e, wbt=wbt, c0=c0, c1=c1: e.tensor_copy(out=fvs[:, c0:c1], in_=wbt[0:8, 0:c1 - c0]), r=[wbb], w=[B_fvs])
            kb.dma("sp", fvd.ap(), fvs[:], r=[B_fvs], w=[B_fvd])
            for dst, Bdst, off in ((F0, B_F0, 0), (F1, B_F1, 128), (FW, B_FW, 384), (T0, B_T0, 640), (T1, B_T1, 896)):
                src = bass.AP(tensor=fvd, offset=off, ap=[[1, 128], [NSEQ, 8], [1, 128]])
                kb.dma("sp", hk[:], src, r=[B_fvd], w=[B_hk])
                for hh in range(2):
                    wbt, wbb = next_wb()
                    kb.op("pe", lambda e, wbt=wbt, hh=hh: e.matmul(wbt[:, :], lhsT=j32[:, :], rhs=hk[:, hh * 4:(hh + 1) * 4, :], start=True, stop=True),
                          r=[B_j32, B_hk], w=[wbb])
                    kb.op("dve", lambda e, wbt=wbt, hh=hh, dst=dst: e.tensor_copy(out=dst[:, hh * 4:(hh + 1) * 4, :], in_=wbt[:, :].rearrange("p (a b) -> p a b", a=4)),
                          r=[wbb], w=[Bdst])
            kb.op("dve", lambda e: e.tensor_copy(out=GN[:, :, 0:8], in_=T1[:, :, 15:128:16]), r=[B_T1], w=[B_GN])
            kb.op("dve", lambda e: e.tensor_copy(out=GN[:, :, 8:16], in_=T0[:, :, 15:128:16]), r=[B_T0], w=[B_GN])
            if do_sample:
                pti, B_pti = sb2("pti", [128, 256], I32)
                ptf, B_ptf = sb2("ptf", [128, 256], F32)
                iop, B_iop = sb2("iop", [128, 1], F32)
                kb.dma("sp", pti[:], bass.AP(tensor=ptab, offset=0, ap=[[0, 128], [1, 256]]), w=[B_pti])
                kb.dma("sp", iop[:], c_iota.ap(), w=[B_iop])
                kb.op("dve", lambda e: e.tensor_copy(out=ptf[:], in_=pti[:]), r=[B_pti], w=[B_ptf])
                kb.op("dve", lambda e: e.tensor_scalar(out=ptf[:], in0=ptf[:], scalar1=128.0, scalar2=iop[:, 0:1], op0=ALU.mult, op1=ALU.add), r=[B_ptf, B_iop], w=[B_ptf])
                kb.op("dve", lambda e: e.tensor_copy(out=IDX[:], in_=ptf[:]), r=[B_ptf], w=[B_IDX])
                rhsF, B_rhsF = sb2("rhsF", [9, 64], F32)
                selb, B_selb = sb2("selb", [9, 16, 128], F32)
                kb.dma("sp", selb[:], c_selb.ap(), w=[B_selb])
                kb.dma("sp", rhsF[8:9, :], c_neg64.ap(), w=[B_rhsF])
                kb.op("dve", lambda e: e.tensor_copy(out=rhsF[0:8, :].rearrange("p (h q) -> p h q", h=8), in_=F0[0:8, :, 0:8]), r=[B_F0], w=[B_rhsF])
                for b in range(16):
                    wbt, wbb = next_wb()
                    kb.op("pe", lambda e, wbt=wbt, b=b: e.matmul(wbt[:, 0:64], lhsT=selb[:, b, :], rhs=rhsF[:, :], start=True, stop=True), r=[B_selb, B_rhsF], w=[wbb])
                    kb.op("dve", lambda e, wbt=wbt, b=b: e.tensor_copy(out=FSB[:, b, :], in_=wbt[:, 0:64]), r=[wbb], w=[B_FSB])
            kb.barrier()
            if stage == 1:
                kb.dma("sp", y_p[0:128, 0:1024], F0[:], r=[B_F0], final=True)
                kb.dma("sp", y_p[128:256, 0:1024], F1[:], r=[B_F1], final=True)
                kb.dma("sp", y_p[256:384, 0:1024], FW[:], r=[B_FW], final=True)
                kb.dma("sp", y_p[384:512, 0:128], GN[:], r=[B_GN], final=True)
                kb.emit()
                return nc

        for l in range(L):
            last = (l == L - 1)
            with contextlib.ExitStack() as stA:
                def sa(name, shape, dt):
                    return stA.enter_context(nc.sbuf_tensor(un(name), list(shape), dt)), Buf(name)
                w_in_sb, B_win = sa("w_in_sb", [128, 8, INC], BF16)
                w_out_sb, B_wout = sa("w_out_sb", [128, 8, D], BF16)
                gmix, B_gmix = sa("gmix", [128, D], F32)
                wk_bd, B_wk = sa("wk_bd", [128, 32, 128], BF16)
                wv_bd, B_wv = sa("wv_bd", [128, 32, 128], BF16)
                peT, B_peT = sa("peT", [128, 2, 32], BF16)
                biasKV, B_bkv = sa("biasKV", [128, 2], F32)
                wpool_sb, B_wpool = sa("wpool_sb", [128, 4, 128], BF16)
                pscale, B_pscale = sa("pscale", [128, 4], F32)
                KE = [sa("KE%d" % i, [128, TE], BF16) for i in range(2)]
                Vsel, B_Vsel = sa("Vsel", [128, max(NT, 16), 2, 65], BF16)
                KTwin, B_KTwin = sa("KTwin", [128, 8, 128], BF16)
                Vwin, B_Vwin = sa("Vwin", [128, 8, 2, 65], BF16)
                KcT, B_KcT = sa("KcT", [128, 144], BF16)
                VcT, B_VcT = sa("VcT", [128, 144], BF16)
                compKT, B_cKT = sa("compKT", [128, 256], BF16)
                compVT, B_cVT = sa("compVT", [128, 256], BF16)
                compV, B_cV = sa("compV", [128, 2, 128], BF16)
                xt = [sa("xt%d" % i, [128, D], F32) for i in range(2)]
                ss, B_ss = sa("ss", [128, 2], F32)
                hb2 = [sa("h_bf%d" % i, [128, D], BF16) for i in range(2)]
                hT2 = [sa("hT%d" % i, [128, 8, 128], BF16) for i in range(2)]
                ssl = [sa("ssn%d" % i, [128, 2], F32) for i in range(2)]
                QN = [sa("QN%d" % i, [128, 4, 128], BF16) for i in range(2)]
                QZ = [sa("QZ%d" % i, [128, 4, 128], BF16) for i in range(2)]
                kvrow, B_kv = sa("kvrow", [128, 768], F32)
                gates, B_gates = sa("gates", [128, 24], F32)
                u_f, B_uf = sa("u_f", [128, 512], F32)
                u_bf = [sa("u_bf%d" % i, [128, 512], BF16) for i in range(2)]
                Scmp2 = [sa("Scmp%d" % i, [128, 4, 256], F32) for i in range(2)]
                rsum2 = [sa("rsum%d" % i, [128, 8], F32) for i in range(2)]
                Pn2 = [sa("Pn%d" % i, [128, 4, 256], BF16) for i in range(2)]
                Psum_g, B_Psg = sa("Psum_g", [128, 264], F32)
                imp, B_imp = sa("imp", [128, 64], F32)
                wk64, B_wk64 = sa("wk64", [128, 64], F32)
                m8, B_m8 = sa("m8", [128, 16], F32)
                NMq2 = [sa("NMq%d" % i, [128, 128], BF16) for i in range(2)]
                PTc, B_PTc = sa("PTc", [128, 2, 4, 128], BF16)
                Sadd = [sa("Sadd%d" % i, [128, 512], F32) for i in range(2)]
                PT = [sa("PT%d" % i, [128, 512], BF16) for i in range(3)]
                o_att, B_oatt = sa("o_att", [128, 512], F32)
                o_tmp, B_otmp = sa("o_tmp", [128, 512], F32)
                coef, B_coef = sa("coef", [128, 8], F32)
                o_bf, B_obf = sa("o_bf", [128, 512], BF16)
                catT, B_catT = sa("catT", [128, 8, 128], BF16)
                poolT, B_poolT = sa("poolT", [128, 4, 128], BF16)
                xnew = [sa("xnew%d" % i, [128, D], F32) for i in range(1)]
                if do_sample:
                    Gb = [sa("Gb%d" % i, [128, 8, 256], BF16) for i in range(2)]
                    VsT, B_VsT = sa("VsT", [128, 2048], BF16)
                    KsT, B_KsT = sa("KsT", [128, 2048], BF16)
                    KnT, B_KnT = sa("KnT", [128, 128], BF16)
                    Vn, B_Vn = sa("Vn", [128, 2, 2, 65], BF16)
                    o_att_s, B_oatts = sa("o_att_s", [128, 512], F32)
                    gates_s, B_gates_s = sa("gates_s", [8, 16, 24], F32)
                    sp_bf, B_spbf = sa("sp_bf", [128, 2, 512], BF16)
                if stage == 98:
                    print("sbuf bytes remaining after phase-AB allocs:", nc.sbuf_bytes_remaining)
                rot2 = {"sadd": 0, "pt": 0}

                for k in range(8):
                    for g in range(2):
                        kb.dma("pool", w_in_sb[:, k, 0:512].rearrange("r (p g d) -> r p g d", p=4, g=2)[:, :, g, :],
                               w_in[l, k * 128:(k + 1) * 128, g * 256:(g + 1) * 256].rearrange("r (p d) -> r p d", p=4), w=[B_win])
                    kb.dma("pool", w_in_sb[:, k, 512:INC], w_in[l, k * 128:(k + 1) * 128, 512:INC], w=[B_win])
                    kb.dma("pool", w_out_sb[:, k, :], w_out[l, k * 128:(k + 1) * 128, :], w=[B_wout])
                kb.dma("sp", gmix[:], bass.AP(tensor=norm_mix, offset=l * D, ap=[[0, 128], [1, D]]), w=[B_gmix])
                kb.op("pool", lambda e: e.memset(wk_bd[:], 0.0), w=[B_wk])
                kb.op("pool", lambda e: e.memset(wv_bd[:], 0.0), w=[B_wv])
                for g in range(2):
                    kb.dma("pool", wk_bd[64 * g:64 * g + 64, :, 64 * g:64 * g + 64], w_cmp[l, 0].rearrange("l d e -> d l e"), w=[B_wk])
                    kb.dma("pool", wv_bd[64 * g:64 * g + 64, :, 64 * g:64 * g + 64], w_cmp[l, 1].rearrange("l d e -> d l e"), w=[B_wv])
                    for j in range(2):
                        kb.dma("pool", peT[64 * g:64 * g + 64, j, :], cmp_pos[l, :, j, :].rearrange("l d -> d l"), w=[B_peT],
                               allow_slow_non_contiguous=True)
                kb.dma("pool", wpool_sb[:], w_pool[l].rearrange("k c d -> c k d"), w=[B_wpool])
                kb.dma("sp", pscale[:], pool_scale[l].rearrange("(k d) -> d k", k=4), w=[B_pscale], allow_slow_non_contiguous=True)
                for j, (wbd, Bw) in enumerate(((wk_bd, B_wk), (wv_bd, B_wv))):
                    wbt, wbb = next_wb()
                    for ll in range(32):
                        kb.op("pe", lambda e, wbt=wbt, wbd=wbd, ll=ll, j=j: e.matmul(wbt[:, 0:1], lhsT=wbd[:, ll, :], rhs=peT[:, j, ll:ll + 1], start=(ll == 0), stop=(ll == 31)),
                              r=[Bw, B_peT], w=[wbb])
                    kb.op("dve", lambda e, wbt=wbt, j=j: e.tensor_copy(out=biasKV[:, j:j + 1], in_=wbt[:, 0:1]), r=[wbb], w=[B_bkv])
                kb.op("pool", lambda e: e.memset(Vsel[:, :, :, 64:65], 1.0), w=[B_Vsel])
                kb.op("pool", lambda e: e.memset(Vwin[:, :, :, 64:65], 1.0), w=[B_Vwin])
                kb.op("pool", lambda e: e.memset(compVT[:], 0.0), w=[B_cVT])
                kb.op("pool", lambda e: e.memset(compKT[:], 0.0), w=[B_cKT])
                kb.op("pool", lambda e: e.memset(compV[:], 0.0), w=[B_cV])
                kb.op("pool", lambda e: e.memset(KcT[:], 0.0), w=[B_KcT])
                kb.op("pool", lambda e: e.memset(VcT[:], 0.0), w=[B_VcT])
                kb.op("pool", lambda e: e.memset(Psum_g[:], 0.0), w=[B_Psg])
                for g in range(2):
                    kb.op("pool", lambda e: e.memset(NMq2[g][0][:], 0.0), w=[NMq2[g][1]])
                    kb.op("pool", lambda e: e.memset(QZ[g][0][:], 0.0), w=[QZ[g][1]])
                    kb.dma("sp", KE[g][0][64:128, :], c_ex.ap(), w=[KE[g][1]])

                if stage == 2:
                    kb.dma("sp", y_p[0:128, 0:2], biasKV[:], r=[B_bkv], final=True)
                    kb.emit()
                    return nc

                def prep_norm(xsrc_ap, B_src, par):
                    xtt, B_xt = xt[par]
                    h_bf, B_h = hb2[par]
                    ss, B_ss = ssl[par]
                    kb.dma("sp", xtt[:], xsrc_ap, r=[B_src] if B_src is not None else [], w=[B_xt])
                    kb.op("act", lambda e: e.activation(out=h_bf[:], in_=xtt[:], func=AF.Square, accum_out=ss[:, 0:1]), r=[B_xt], w=[B_h, B_ss])
                    kb.op("dve", lambda e: e.tensor_scalar(out=ss[:, 1:2], in0=ss[:, 0:1], scalar1=1.0 / D, scalar2=EPS, op0=ALU.mult, op1=ALU.add), r=[B_ss], w=[B_ss])
                    kb.op("act", lambda e: e.activation(out=ss[:, 1:2], in_=ss[:, 1:2], func=AF.Sqrt), r=[B_ss], w=[B_ss])
                    kb.op("dve", lambda e: e.reciprocal(out=ss[:, 1:2], in_=ss[:, 1:2]), r=[B_ss], w=[B_ss])
                    kb.op("dve", lambda e: e.scalar_tensor_tensor(out=h_bf[:], in0=xtt[:], scalar=ss[:, 1:2], in1=gmix[:], op0=ALU.mult, op1=ALU.mult),
                          r=[B_xt, B_ss, B_gmix], w=[B_h])

                def prep_T(par):
                    h_bf, B_h = hb2[par]
                    hT, B_hT = hT2[par]
                    tbt, tbb = next_tb()
                    for k in range(8):
                        kb.op("pe", lambda e, k=k: e.transpose(out=tbt[:, k * 128:(k + 1) * 128], in_=h_bf[:, k * 128:(k + 1) * 128], identity=ident[:]),
                              r=[B_h, B_ident], w=[tbb])
                    kb.op("act", lambda e: e.copy(out=hT[:].rearrange("p a b -> p (a b)"), in_=tbt[:, :]), r=[tbb], w=[B_hT])

                def project(par, ksel_col):
                    hT, B_hT = hT2[par]
                    wbt, wbb = next_wb()
                    for p in range(4):
                        for k in range(8):
                            lhs = w_in_sb[:, k, p * 128:(p + 1) * 128]
                            kb.op("pe", lambda e, lhs=lhs, p=p, k=k: e.matmul(wbt[:, p * 128:(p + 1) * 128], lhsT=lhs, rhs=hT[:, k, :], start=(k == 0), stop=(k == 7)),
                                  r=[B_win, B_hT], w=[wbb])
                    kb.op("act", lambda e: e.activation(out=QZ[0][0][0:64, :, :].rearrange("p a b -> p (a b)"), in_=wbt[0:64, :], func=AF.Copy, scale=0.125), r=[wbb], w=[QZ[0][1]])
                    kb.op("act", lambda e: e.activation(out=QZ[1][0][64:128, :, :].rearrange("p a b -> p (a b)"), in_=wbt[64:128, :], func=AF.Copy, scale=0.125), r=[wbb], w=[QZ[1][1]])
                    kb.op("dve", lambda e: e.tensor_scalar(out=QN[0][0][0:64, :, :].rearrange("p a b -> p (a b)"), in0=wbt[0:64, :], scalar1=0.125, scalar2=None, op0=ALU.mult), r=[wbb], w=[QN[0][1]])
                    wbx, wbxb = next_wb()
                    for p in range(4):
                        for k in range(8):
                            kb.op("pe", lambda e: e.matmul(wbx[0:64, p * 128:(p + 1) * 128], lhsT=w_in_sb[:, k, p * 128 + 64:(p + 1) * 128], rhs=hT[:, k, :], start=(k == 0), stop=(k == 7)),
                                  r=[B_win, B_hT], w=[wbxb])
                    kb.op("act", lambda e: e.activation(out=QN[1][0][0:64, :, :].rearrange("p a b -> p (a b)"), in_=wbx[0:64, :], func=AF.Copy, scale=0.125), r=[wbxb], w=[QN[1][1]])
                    wby, wbyb = next_wb()
                    for g in range(2):
                        for k in range(8):
                            kb.op("pe", lambda e: e.matmul(wby[0:64, g * 128:(g + 1) * 128], lhsT=w_in_sb[:, k, 768 + 64 * g:768 + 64 * g + 64], rhs=hT[:, k, :], start=(k == 0), stop=(k == 7)),
                                  r=[B_win, B_hT], w=[wbyb])
                    kb.op("dve", lambda e: e.tensor_copy(out=KE[0][0][0:64, ksel_col:ksel_col + 128], in_=wby[0:64, 0:128]), r=[wbyb], w=[KE[0][1]])
                    kb.op("act", lambda e: e.copy(out=KE[1][0][0:64, ksel_col:ksel_col + 128], in_=wby[0:64, 128:256]), r=[wbyb], w=[KE[1][1]])
                    wbt2, wbb2 = next_wb()
                    for i, c0 in ((0, 512), (2, 1024), (3, 640)):
                        for k in range(8):
                            kb.op("pe", lambda e, i=i, c0=c0, k=k: e.matmul(wbt2[:, i * 128:(i + 1) * 128], lhsT=w_in_sb[:, k, c0:c0 + 128], rhs=hT[:, k, :], start=(k == 0), stop=(k == 7)),
                                  r=[B_win, B_hT], w=[wbb2])
                    zc = []
                    for (c0, c1) in ((512, 1024), (1024, 1304), (1304, 1816)):
                        zt, zb = next_wb()
                        for k in range(8):
                            kb.op("pe", lambda e, zt=zt, c0=c0, c1=c1, k=k: e.matmul(zt[:, 0:c1 - c0], lhsT=hT[:, k, :], rhs=w_in_sb[:, k, c0:c1], start=(k == 0), stop=(k == 7)),
                                  r=[B_win, B_hT], w=[zb])
                        zc.append((zt, zb))
                    return wbt2, wbb2, zc

                def evac_tokmajor(zc, ub):
                    (zA, bA), (zB, bB), (zC, bC) = zc
                    kb.op("act", lambda e: e.copy(out=kvrow[:, 0:512], in_=zA[:, 0:512]), r=[bA], w=[B_kv])
                    kb.op("dve", lambda e: e.tensor_copy(out=kvrow[:, 512:768], in_=zB[:, 0:256]), r=[bB], w=[B_kv])
                    kb.op("act", lambda e: e.activation(out=gates[:], in_=zB[:, 256:280], func=AF.Sigmoid), r=[bB], w=[B_gates])
                    kb.op("dve", lambda e: e.tensor_copy(out=u_f[:], in_=zC[:, 0:512]), r=[bC], w=[B_uf])
                    kb.op("act", lambda e: e.copy(out=ub[0][:], in_=zC[:, 0:512]), r=[bC], w=[ub[1]])

                def attend(nq, qsl, key_tiles, Bo, g, first_flag, fsl=None):
                    ot, ob = OB[g]
                    n = 4 * nq
                    if fsl is None:
                        fsl = qsl
                    ntl = len(key_tiles)

                    def stage_a(ti):
                        kt_ap, v_ap, nk, Fb, mask, bufs = key_tiles[ti]
                        wbt, wbb = next_wb()
                        qsrc, qb = (QN[g] if mask else QZ[g])
                        rhs = qsrc[:, :, qsl]
                        kb.op("pe", lambda e: e.matmul(wbt[0:nk, 0:n].rearrange("p (a b) -> p a b", a=4), lhsT=kt_ap, rhs=rhs, start=True, stop=True),
                              r=[qb] + bufs, w=[wbb])
                        if Fb is not None:
                            Ft, FB_ = Fb
                            rot2["sadd"] += 1
                            sat, sab = Sadd[rot2["sadd"] % 2]
                            kb.op("dve", lambda e: e.tensor_tensor(out=sat[0:nk, 0:n].rearrange("p (a b) -> p a b", a=4), in0=wbt[0:nk, 0:n].rearrange("p (a b) -> p a b", a=4),
                                                                   in1=Ft[0:nk, 4 * g:4 * g + 4, fsl], op=ALU.add),
                                  r=[wbb, FB_], w=[sab])
                            return sat, sab
                        return wbt, wbb

                    def stage_b(ti, src):
                        kt_ap, v_ap, nk, Fb, mask, bufs = key_tiles[ti]
                        st_t, st_b = src
                        rot2["pt"] += 1
                        ptt, ptb = PT[rot2["pt"] % 3]
                        kb.op("act", lambda e: e.activation(out=ptt[0:nk, 0:n], in_=st_t[0:nk, 0:n], func=AF.Exp), r=[st_b], w=[ptb])
                        for h in range(4):
                            st_ = first_flag[0]
                            first_flag[0] = False
                            kb.op("pe", lambda e: e.matmul(ot[0:nq, h * 65:(h + 1) * 65], lhsT=ptt[0:nk, h * nq:(h + 1) * nq], rhs=v_ap, start=st_, stop=(ti == ntl - 1), skip_group_check=True),
                                  r=[ptb] + bufs, w=[ob])

                    pend = []
                    for ti in range(ntl):
                        pend.append((ti, stage_a(ti)))
                        if len(pend) > 2:
                            stage_b(*pend.pop(0))
                    while pend:
                        stage_b(*pend.pop(0))

                def combine(nq, g, br, first, gsrc=None):
                    ot, ob = OB[g]
                    ov = ot[0:nq, 0:260].rearrange("p (h d) -> p h d", h=4)
                    cf = coef[0:nq, 4 * g:4 * g + 4]
                    gt_ap, B_gt = gsrc if gsrc is not None else (gates[0:nq, :], B_gates)
                    gv = gt_ap.rearrange("p (h b) -> p h b", b=3)[:, 4 * g:4 * g + 4, br]
                    if br == 0:
                        kb.op("dve", lambda e: e.tensor_copy(out=cf, in_=gv), r=[B_gt], w=[B_coef])
                    else:
                        kb.op("dve", lambda e: e.reciprocal(out=cf, in_=ov[:, :, 64]), r=[ob], w=[B_coef])
                        kb.op("dve", lambda e: e.tensor_tensor(out=cf, in0=cf, in1=gv, op=ALU.mult), r=[B_coef, B_gt], w=[B_coef])
                    first = False
                    dst = o_tmp[0:nq, 256 * g:256 * g + 256].rearrange("p (h d) -> p h d", h=4)
                    Bd = B_otmp
                    kb.op("dve", lambda e: e.tensor_tensor(out=dst, in0=ov[:, :, 0:64], in1=bc_ap(cf, [64]), op=ALU.mult), r=[ob, B_coef], w=[Bd])
                    if not first:
                        oa = o_att[0:nq, 256 * g:256 * g + 256]
                        kb.op("dve", lambda e: e.tensor_tensor(out=oa, in0=oa, in1=o_tmp[0:nq, 256 * g:256 * g + 256], op=ALU.add), r=[B_oatt, B_otmp], w=[B_oatt])

                def cmp_1a(nq, qsl, ncv, gn_lo, gn_hi, g):
                    Scmp, B_Scmp = Scmp2[g]
                    rsum, B_rsum = rsum2[g]
                    for hp in range(2):
                        wbt, wbb = next_wb()
                        for hh in range(2):
                            h = hp * 2 + hh
                            kb.op("pe", lambda e, wbt=wbt, h=h, hh=hh: e.matmul(wbt[0:nq, hh * 256:hh * 256 + ncv], lhsT=QZ[g][0][:, h, qsl], rhs=compKT[:, 0:ncv], start=(hh == 0), stop=(hh == 1), skip_group_check=True),
                                  r=[QZ[g][1], B_cKT], w=[wbb])
                        ngn = gn_hi - gn_lo
                        wv = wbt[0:nq, :].rearrange("p (a b) -> p a b", a=2)
                        kb.op("dve", lambda e, wv=wv, hp=hp: e.tensor_tensor(out=wv[:, :, ncv - ngn:ncv], in0=wv[:, :, ncv - ngn:ncv], in1=GN[0:nq, 4 * g + 2 * hp:4 * g + 2 * hp + 2, gn_lo:gn_hi], op=ALU.add),
                              r=[wbb, B_GN], w=[wbb])
                        for hh in range(2):
                            h = hp * 2 + hh
                            kb.op("act", lambda e, wbt=wbt, h=h, hh=hh: e.activation(out=Scmp[0:nq, h, 0:ncv], in_=wbt[0:nq, hh * 256:hh * 256 + ncv], func=AF.Exp, accum_out=rsum[0:nq, h:h + 1]),
                                  r=[wbb], w=[B_Scmp, B_rsum])

                def cmp_1b(nq, ncv, t_fc, g):
                    Pn, B_Pn = Pn2[g]
                    Scmp, B_Scmp = Scmp2[g]
                    rsum, B_rsum = rsum2[g]
                    kb.op("dve", lambda e: e.tensor_scalar_max(out=rsum[0:nq, 0:4], in0=rsum[0:nq, 0:4], scalar1=1e-30), r=[B_rsum], w=[B_rsum])
                    kb.op("dve", lambda e: e.reciprocal(out=rsum[0:nq, 4:8], in_=rsum[0:nq, 0:4]), r=[B_rsum], w=[B_rsum])
                    kb.op("dve", lambda e: e.tensor_tensor(out=Scmp[0:nq, :, 0:ncv], in0=Scmp[0:nq, :, 0:ncv], in1=bc_ap(rsum[0:nq, 4:8], [ncv]), op=ALU.mult),
                          r=[B_Scmp, B_rsum], w=[B_Scmp])
                    kb.op("dve", lambda e: e.tensor_tensor(out=Psum_g[0:nq, 1:1 + ncv], in0=Scmp[0:nq, 0, 0:ncv], in1=Scmp[0:nq, 1, 0:ncv], op=ALU.add), r=[B_Scmp], w=[B_Psg])
                    for h in (2, 3):
                        kb.op("dve", lambda e, h=h: e.tensor_tensor(out=Psum_g[0:nq, 1:1 + ncv], in0=Psum_g[0:nq, 1:1 + ncv], in1=Scmp[0:nq, h, 0:ncv], op=ALU.add), r=[B_Scmp, B_Psg], w=[B_Psg])
                    kb.op("dve", lambda e: e.tensor_tensor(out=imp[0:nq, :], in0=Psum_g[0:nq, 0:256:4], in1=Psum_g[0:nq, 1:257:4], op=ALU.add), r=[B_Psg], w=[B_imp])
                    for o in (2, 3, 4):
                        kb.op("dve", lambda e, o=o: e.tensor_tensor(out=imp[0:nq, :], in0=imp[0:nq, :], in1=Psum_g[0:nq, o:o + 256:4], op=ALU.add), r=[B_Psg, B_imp], w=[B_imp])
                    kb.op("dve", lambda e: e.tensor_tensor(out=imp[0:nq, :], in0=imp[0:nq, :], in1=FC[0:nq, 64 - 2 * t_fc:128 - 2 * t_fc], op=ALU.add), r=[B_imp, B_FC], w=[B_imp])
                    kb.op("dve", lambda e: e.tensor_copy(out=imp[0:nq, 0:1], in_=big9[0:nq, :]), r=[B_imp, B_big9], w=[B_imp])
                    nround = (NSEL + 7) // 8
                    src = imp
                    for rd in range(nround):
                        kb.op("dve", lambda e, src=src, rd=rd: e.max(out=m8[0:nq, rd * 8:rd * 8 + 8], in_=src[0:nq, :]), r=[B_imp, B_wk64], w=[B_m8])
                        if rd < nround - 1:
                            kb.op("dve", lambda e, src=src, rd=rd: e.match_replace(out=wk64[0:nq, :], in_to_replace=m8[0:nq, rd * 8:rd * 8 + 8], in_values=src[0:nq, :], imm_value=-3e9),
                                  r=[B_m8, B_imp, B_wk64], w=[B_wk64])
                            src = wk64
                    kb.op("dve", lambda e: e.tensor_tensor(out=wk64[0:nq, :], in0=imp[0:nq, :], in1=bc_ap(m8[0:nq, NSEL - 1:NSEL], [])[:, 0:1].to_broadcast([nq, 64]) if False else bass.AP(tensor=m8[:].tensor, offset=m8[0:nq, NSEL - 1:NSEL].offset, ap=[list(m8[0:nq, :].ap[0]), [0, 64]]), op=ALU.is_ge), r=[B_imp, B_m8], w=[B_wk64])
                    kb.op("dve", lambda e: e.tensor_scalar(out=NMq2[g][0][0:nq, 64:128], in0=wk64[0:nq, :], scalar1=-1.0, scalar2=-NEG, op0=ALU.add, op1=ALU.mult), r=[B_wk64], w=[NMq2[g][1]])
                    kb.op("act", lambda e: e.copy(out=Pn[0:nq, :, 0:ncv], in_=Scmp[0:nq, :, 0:ncv]), r=[B_Scmp], w=[B_Pn])

                def cmp_part2(nq, qsl, ncv, g, gsrc=None):
                    Pn, B_Pn = Pn2[g]
                    tbt, tbb = next_tb()
                    kb.op("pe", lambda e: e.transpose(out=tbt[0:128, 0:nq], in_=NMq2[g][0][0:nq, :], identity=ident[0:nq, 0:nq]), r=[NMq2[g][1], B_ident], w=[tbb])
                    tsrc = tbt[64:128, 0:nq]
                    kb.op("act", lambda e: e.copy(out=QN[g][0][64:128, :, qsl], in_=bass.AP(tensor=tsrc.tensor, offset=tsrc.offset, ap=[list(tsrc.ap[0]), [0, 4], [1, nq]])), r=[tbb], w=[QN[g][1]])
                    nct = (ncv + 127) // 128
                    ot, ob = OB[g]
                    for ct in range(nct):
                        cw = min(128, ncv - ct * 128)
                        tbt, tbb = next_tb()
                        for h in range(4):
                            kb.op("pe", lambda e, tbt=tbt, h=h, ct=ct, cw=cw: e.transpose(out=tbt[0:cw, h * 128:h * 128 + nq], in_=Pn[0:nq, h, ct * 128:ct * 128 + cw], identity=ident[0:nq, 0:nq]),
                                  r=[B_Pn, B_ident], w=[tbb])
                        kb.op("act", lambda e, tbt=tbt, ct=ct, cw=cw: e.copy(out=PTc[0:cw, ct, :, 0:nq], in_=tbt[0:cw, 0:512].rearrange("p (a b) -> p a b", a=4)[:, :, 0:nq]),
                              r=[tbb], w=[B_PTc])
                    first = True
                    for ct in range(nct):
                        cw = min(128, ncv - ct * 128)
                        for h in range(4):
                            kb.op("pe", lambda e, h=h, ct=ct, cw=cw, first=first, lastk=(ct == nct - 1): e.matmul(ot[0:nq, h * 65:h * 65 + 64], lhsT=PTc[0:cw, ct, h, 0:nq], rhs=compV[0:cw, ct, 64 * g:64 * g + 64], start=first, stop=lastk, skip_group_check=True),
                                  r=[B_PTc, B_cV], w=[ob])
                            first = False
                    combine(nq, g, 0, True, gsrc)

                def finish_tile(nq, xtt, B_xt, par, ub_cur, prev_pool, kinds, xdst_ap, B_xdst, oa=None):
                    oat, B_oat = oa if oa is not None else (o_att, B_oatt)
                    kb.op("act", lambda e: e.copy(out=o_bf[:], in_=oat[:]), r=[B_oat], w=[B_obf])
                    tbt, tbb = next_tb()
                    for k in range(4):
                        kb.op("pe", lambda e, k=k: e.transpose(out=tbt[:, k * 128:(k + 1) * 128], in_=o_bf[:, k * 128:(k + 1) * 128], identity=ident[:]), r=[B_obf, B_ident], w=[tbb])
                    kb.op("act", lambda e: e.copy(out=catT[:, 0:4, :].rearrange("p a b -> p (a b)"), in_=tbt[:, 0:512]), r=[tbb], w=[B_catT])
                    wbt, wbb = next_wb()
                    for k in range(4):
                        srcs = [(ub_cur[0][:, k * 128:(k + 1) * 128], PB[:, kinds[0] * 4 + k, :], ub_cur[1])]
                        for (pa, pk, pbuf) in prev_pool:
                            srcs.append((pa[:, k * 128:(k + 1) * 128], PB[:, pk * 4 + k, :], pbuf))
                        for si, (la, ra, bb) in enumerate(srcs):
                            kb.op("pe", lambda e, la=la, ra=ra, k=k, si=si, ns=len(srcs): e.matmul(wbt[:, k * 128:(k + 1) * 128], lhsT=la, rhs=ra, start=(si == 0), stop=(si == ns - 1)),
                                  r=[bb, B_PB], w=[wbb])
                    kb.op("act", lambda e: e.copy(out=poolT[:].rearrange("p a b -> p (a b)"), in_=wbt[:, :]), r=[wbb], w=[B_poolT])
                    wbt2, wbb2 = next_wb()
                    for k in range(4):
                        kb.op("pe", lambda e, k=k: e.matmul(wbt2[:, k * 128:(k + 1) * 128], lhsT=wpool_sb[:, k, :], rhs=poolT[:, k, :], start=True, stop=True), r=[B_wpool, B_poolT], w=[wbb2])
                    for k in range(4):
                        kb.op("act", lambda e, k=k: e.activation(out=catT[:, 4 + k, :], in_=wbt2[:, k * 128:(k + 1) * 128], func=AF.Identity, scale=pscale[:, k:k + 1]), r=[wbb2, B_pscale], w=[B_catT])
                    xn, B_xn = xnew[0]
                    for hf in range(2):
                        wo, wob = next_wb()
                        for k in range(8):
                            kb.op("pe", lambda e, wo=wo, k=k, hf=hf: e.matmul(wo[:, :], lhsT=catT[:, k, :], rhs=w_out_sb[:, k, hf * 512:(hf + 1) * 512], start=(k == 0), stop=(k == 7)),
                                  r=[B_catT, B_wout], w=[wob])
                        kb.op("dve", lambda e, wo=wo, hf=hf: e.tensor_tensor(out=xn[:, hf * 512:(hf + 1) * 512], in0=wo[:, :], in1=xtt[:, hf * 512:(hf + 1) * 512], op=ALU.add), r=[wob, B_xt], w=[B_xn])
                    kb.dma("sp", xdst_ap, xn[:], r=[B_xn], w=[B_xdst])

                def prep_for(tt):
                    if tt < NT:
                        return ((xp if l == 0 else xscr_p)[tt * 128:(tt + 1) * 128, :], None if l == 0 else B_xscr_p[tt])
                    if tt == NT and do_sample:
                        return ((xs if l == 0 else xscr_s)[:, :], None if l == 0 else B_xscr_s)
                    return None

                a0 = prep_for(0)
                prep_norm(a0[0], a0[1], 0)
                prep_T(0)
                for t in range(NT):
                    par = t % 2
                    fk, fkb, zc = project(par, t * 128)
                    nxt = prep_for(t + 1)
                    if nxt is not None:
                        prep_norm(nxt[0], nxt[1], 1 - par)
                    xtt, B_xt = xt[par]
                    ring = t % 8
                    kb.op("act", lambda e: e.copy(out=KTwin[:, ring, :], in_=fk[:, 256:384]), r=[fkb], w=[B_KTwin])
                    kb.op("dve", lambda e: e.tensor_copy(out=KcT[:, 16:144], in_=fk[:, 0:128]), r=[fkb], w=[B_KcT])
                    kb.op("act", lambda e: e.copy(out=VcT[:, 16:144], in_=fk[:, 384:512]), r=[fkb], w=[B_VcT])
                    evac_tokmajor(zc, u_bf[par])
                    if stage == 3:
                        kb.dma("sp", ncmp_p[l, t * 128:(t + 1) * 128, :], kvrow[:, 0:256], r=[B_kv], final=True)
                        kb.emit()
                        return nc
                    kb.op("pool", lambda e: e.tensor_copy(out=Vsel[:, t, :, 0:64], in_=kvrow[:, 384:512].rearrange("p (g d) -> p g d", g=2)), r=[B_kv], w=[B_Vsel])
                    kb.op("pool", lambda e: e.tensor_copy(out=Vwin[:, ring, :, 0:64], in_=kvrow[:, 640:768].rearrange("p (g d) -> p g d", g=2)), r=[B_kv], w=[B_Vwin])
                    kb.dma("sp", ncmp_p[l, t * 128:(t + 1) * 128, :], kvrow[:, 0:256], r=[B_kv], final=True)
                    kb.dma("sp", nsel_p[l, t * 128:(t + 1) * 128, :], kvrow[:, 256:512], r=[B_kv], final=True)
                    if t >= NT - 4:
                        kb.dma("sp", nwin_p[l, (t - (NT - 4)) * 128:(t - (NT - 4) + 1) * 128, :], kvrow[:, 512:768], r=[B_kv], final=True)
                    if t == NT - 1:
                        kb.dma("sp", npool_p[l, :, :], u_f[113:128, :], r=[B_uf], final=True)
                    if do_attn:
                        i0 = 1 if t == 0 else 0
                        c0 = 8 * t - 1 + i0
                        ncn = 8 - i0
                        for (srcT, Bs, wbd, Bw, dstT, Bd, j) in ((KcT, B_KcT, wk_bd, B_wk, compKT, B_cKT, 0), (VcT, B_VcT, wv_bd, B_wv, compVT, B_cVT, 1)):
                            wbt, wbb = next_wb()
                            for ll in range(32):
                                kb.op("pe", lambda e, wbt=wbt, wbd=wbd, srcT=srcT, ll=ll: e.matmul(wbt[:, 0:ncn], lhsT=wbd[:, ll, :], rhs=srcT[:, ll + 16 * i0:ll + 16 * i0 + 16 * (ncn - 1) + 1:16], start=(ll == 0), stop=(ll == 31)),
                                      r=[Bw, Bs], w=[wbb])
                            kb.op("act", lambda e, wbt=wbt, dstT=dstT, j=j: e.activation(out=dstT[:, c0:c0 + ncn], in_=wbt[:, 0:ncn], func=AF.Identity, bias=biasKV[:, j:j + 1]), r=[wbb, B_bkv], w=[Bd])
                        kb.op("dve", lambda e: e.tensor_copy(out=KcT[:, 0:16], in_=KcT[:, 128:144]), r=[B_KcT], w=[B_KcT])
                        kb.op("dve", lambda e: e.tensor_copy(out=VcT[:, 0:16], in_=VcT[:, 128:144]), r=[B_VcT], w=[B_VcT])
                        if nxt is not None:
                            prep_T(1 - par)
                        for ct in sorted(set((c0 // 128, (c0 + ncn - 1) // 128))):
                            tbt, tbb = next_tb()
                            kb.op("pe", lambda e, tbt=tbt, ct=ct: e.transpose(out=tbt[:, 0:128], in_=compVT[:, ct * 128:(ct + 1) * 128], identity=ident[:]), r=[B_cVT, B_ident], w=[tbb])
                            kb.op("dve", lambda e, tbt=tbt, ct=ct: e.tensor_copy(out=compV[:, ct, :], in_=tbt[:, 0:128]), r=[tbb], w=[B_cV])
                        ncv = 8 * t + 7
                        qsl = slice(0, 128)
                        kb.op("pool", lambda e: e.memset(o_att[:], 0.0), w=[B_oatt])
                        for g in range(2):
                            cmp_1a(128, qsl, ncv, max(0, 9 - 8 * t), 16, g)
                        for g in range(2):
                            cmp_1b(128, ncv, t, g)
                        for g in range(2):
                            tiles = []
                            for kt in range(max(0, t - 4), t + 1):
                                Fb = (F0, B_F0) if kt == t else ((F1, B_F1) if kt == t - 1 else ((FW, B_FW) if kt == t - 4 else None))
                                rg = kt % 8
                                tiles.append((KTwin[:, rg, :], Vwin[:, rg, g, :], 128, Fb, False, [B_KTwin, B_Vwin]))
                            attend(128, qsl, tiles, None, g, [True])
                            combine(128, g, 2, False)
                        for g in range(2):
                            cmp_part2(128, qsl, ncv, g)
                        for g in range(2):
                            tiles = []
                            for kt in range(t + 1):
                                Fb = (F0, B_F0) if kt == t else ((F1, B_F1) if kt == t - 1 else None)
                                tiles.append((KE[g][0][:, kt * 128:(kt + 1) * 128], Vsel[:, kt, g, :], 128, Fb, True, [KE[g][1], B_Vsel]))
                            attend(128, qsl, tiles, None, g, [True])
                            combine(128, g, 1, False)
                    else:
                        kb.op("pool", lambda e: e.memset(o_att[:], 0.0), w=[B_oatt])
                        if nxt is not None:
                            prep_T(1 - par)
                    prev_pool = [] if t == 0 else [(u_bf[1 - par][0], 1, u_bf[1 - par][1])]
                    finish_tile(128, xtt, B_xt, par, u_bf[par], prev_pool, (2 if t == 0 else 0,), xscr_p[t * 128:(t + 1) * 128, :], B_xscr_p[t])
                if do_sample:
                    par = NT % 2
                    fk, fkb, zc = project(par, 2048)
                    xtt, B_xt = xt[par]
                    kb.op("act", lambda e: e.copy(out=KnT[:, :], in_=fk[:, 256:384]), r=[fkb], w=[B_KnT])
                    evac_tokmajor(zc, u_bf[par])
                    kb.op("pool", lambda e: e.memset(Vn[:, :, :, 64:65], 1.0), w=[B_Vn])
                    kb.op("pool", lambda e: e.tensor_copy(out=Vn[:, 0, :, 0:64], in_=kvrow[:, 384:512].rearrange("p (g d) -> p g d", g=2)), r=[B_kv], w=[B_Vn])
                    kb.op("pool", lambda e: e.tensor_copy(out=Vn[:, 1, :, 0:64], in_=kvrow[:, 640:768].rearrange("p (g d) -> p g d", g=2)), r=[B_kv], w=[B_Vn])
                    kb.dma("sp", ncmp_s[l, :, :], kvrow[:, 0:256], r=[B_kv], final=True)
                    kb.dma("sp", nsel_s[l, :, :], kvrow[:, 256:512], r=[B_kv], final=True)
                    kb.dma("sp", nwin_s[l, :, 0:504, :], swin[l, :, 8:512, :], final=True)
                    kb.dma("sp", npool_s[l, :, 0:7, :], spool[l, :, 8:15, :], final=True)
                    for b in range(16):
                        kb.dma("sp", nwin_s[l, b, 504:512, :], kvrow[b * 8:(b + 1) * 8, 512:768], r=[B_kv], final=True)
                        kb.dma("sp", npool_s[l, b, 7:15, :], u_f[b * 8:(b + 1) * 8, :], r=[B_uf], final=True)
                    for b in range(16):
                        kb.dma("sp", gates_s[0:8, b, :], gates[b * 8:(b + 1) * 8, :], r=[B_gates], w=[B_gates_s])
                    kb.op("pool", lambda e: e.memset(sp_bf[:], 0.0), w=[B_spbf])
                    for hf in range(2):
                        kb.dma("pool", sp_bf[0:120, hf, :], spool[l, hf * 8:(hf + 1) * 8, :, :].rearrange("b r c -> (b r) c"), w=[B_spbf])
                    kb.op("pool", lambda e: e.memset(Psum_g[:], 0.0), w=[B_Psg])
                    kb.op("pool", lambda e: e.memset(compKT[:], 0.0), w=[B_cKT])
                    kb.op("pool", lambda e: e.memset(compVT[:], 0.0), w=[B_cVT])
                    if do_attn:
                        grot = [0]
                        for b in range(16):
                            qsl = slice(b * 8, b * 8 + 8)
                            for ci, cache in enumerate((ccmp, csel)):
                                for half in range(2):
                                    grot[0] += 1
                                    gbt, gbb = Gb[grot[0] % 2]
                                    for j in range(8):
                                        pg = half * 8 + j
                                        kb.op("pool", lambda e, gbt=gbt, j=j, pg=pg, cache=cache: e.indirect_dma_start(
                                            out=gbt[:, j, :], out_offset=None, in_=cache.ap().rearrange("l r c -> (l r) c"),
                                            in_offset=bass.IndirectOffsetOnAxis(ap=IDX[:, b * 16 + pg:b * 16 + pg + 1], axis=0),
                                            element_offset=l * NPHYS * 128 * 256),
                                            r=[B_IDX], w=[gbb], dma=True)
                                    c0 = half * 1024
                                    if ci == 0:
                                        tbt, tbb = next_tb()
                                        for j in range(8):
                                            kb.op("pe", lambda e, tbt=tbt, gbt=gbt, j=j: e.transpose(out=tbt[:, j * 128:(j + 1) * 128], in_=gbt[:, j, 0:128], identity=ident[:]), r=[gbb, B_ident], w=[tbb])
                                        kb.op("act", lambda e, tbt=tbt, c0=c0: e.copy(out=KsT[:, c0:c0 + 1024], in_=tbt[:, :]), r=[tbb], w=[B_KsT])
                                        tbt2, tbb2 = next_tb()
                                        for j in range(8):
                                            kb.op("pe", lambda e, tbt2=tbt2, gbt=gbt, j=j: e.transpose(out=tbt2[:, j * 128:(j + 1) * 128], in_=gbt[:, j, 128:256], identity=ident[:]), r=[gbb, B_ident], w=[tbb2])
                                        kb.op("dve", lambda e, tbt2=tbt2, c0=c0: e.tensor_copy(out=VsT[:, c0:c0 + 1024], in_=tbt2[:, :]), r=[tbb2], w=[B_VsT])
                                    else:
                                        for g in range(2):
                                            tbt, tbb = next_tb()
                                            for j in range(8):
                                                kb.op("pe", lambda e: e.transpose(out=tbt[0:64, j * 128:(j + 1) * 128], in_=gbt[:, j, 64 * g:64 * g + 64], identity=ident[:]), r=[gbb, B_ident], w=[tbb])
                                            if g == 0:
                                                kb.op("act", lambda e: e.copy(out=KE[g][0][0:64, c0:c0 + 1024], in_=tbt[0:64, :]), r=[tbb], w=[KE[g][1]])
                                            else:
                                                kb.op("dve", lambda e: e.tensor_copy(out=KE[g][0][0:64, c0:c0 + 1024], in_=tbt[0:64, :]), r=[tbb], w=[KE[g][1]])
                                        kb.op("pool", lambda e, gbt=gbt, half=half: e.tensor_copy(out=Vsel[:, half * 8:half * 8 + 8, :, 0:64], in_=gbt[:, :, 128:256].rearrange("p j (g d) -> p j g d", g=2)), r=[gbb], w=[B_Vsel])
                            grot[0] += 1
                            gbt, gbb = Gb[grot[0] % 2]
                            kb.dma("pool", gbt[:, 0:4, :], swin[l, b, :, :].rearrange("(j p) c -> p j c", p=128), w=[gbb])
                            tbt, tbb = next_tb()
                            for j in range(4):
                                kb.op("pe", lambda e, tbt=tbt, gbt=gbt, j=j: e.transpose(out=tbt[:, j * 128:(j + 1) * 128], in_=gbt[:, j, 0:128], identity=ident[:]), r=[gbb, B_ident], w=[tbb])
                            kb.op("act", lambda e, tbt=tbt: e.copy(out=KTwin[:, 0:4, :].rearrange("p a b -> p (a b)"), in_=tbt[:, 0:512]), r=[tbb], w=[B_KTwin])
                            kb.op("pool", lambda e, gbt=gbt: e.tensor_copy(out=Vwin[:, 0:4, :, 0:64], in_=gbt[:, 0:4, 128:256].rearrange("p j (g d) -> p j g d", g=2)), r=[gbb], w=[B_Vwin])
                            for (srcT, Bs, soff, wbd, Bw, dstT, Bd, j) in ((KsT, B_KsT, 0, wk_bd, B_wk, compKT, B_cKT, 0), (VsT, B_VsT, 0, wv_bd, B_wv, compVT, B_cVT, 1)):
                                wbt, wbb = next_wb()
                                for ll in range(32):
                                    kb.op("pe", lambda e, wbt=wbt, wbd=wbd, srcT=srcT, soff=soff, ll=ll: e.matmul(wbt[:, 0:127], lhsT=wbd[:, ll, :], rhs=srcT[:, soff + ll:soff + ll + 16 * 126 + 1:16], start=(ll == 0), stop=(ll == 31)),
                                          r=[Bw, Bs], w=[wbb])
                                kb.op("act", lambda e, wbt=wbt, dstT=dstT, j=j: e.activation(out=dstT[:, 0:127], in_=wbt[:, 0:127], func=AF.Identity, bias=biasKV[:, j:j + 1]), r=[wbb, B_bkv], w=[Bd])
                            tbt, tbb = next_tb()
                            kb.op("pe", lambda e, tbt=tbt: e.transpose(out=tbt[:, 0:128], in_=compVT[:, 0:128], identity=ident[:]), r=[B_cVT, B_ident], w=[tbb])
                            kb.op("dve", lambda e, tbt=tbt: e.tensor_copy(out=compV[:, 0, :], in_=tbt[:, 0:128]), r=[tbb], w=[B_cV])
                            gs_b = (gates_s[0:8, b, :], B_gates_s)
                            kb.op("pool", lambda e: e.memset(o_att[0:32, :], 0.0), w=[B_oatt])
                            for g in range(2):
                                cmp_1a(8, qsl, 128, 0, 9, g)
                            for g in range(2):
                                cmp_1b(8, 128, 16, g)
                            fsb_b = FSB[:, b, :].rearrange("p (h q) -> p h q", h=8)
                            for g in range(2):
                                tiles = []
                                for kt in range(4):
                                    Fb = (FW, B_FW) if kt == 0 else ((F1, B_F1) if kt == 3 else None)
                                    tiles.append((KTwin[:, kt, :], Vwin[:, kt, g, :], 128, Fb, False, [B_KTwin, B_Vwin]))
                                tiles.append((KnT[:, :], Vn[:, 1, g, :], 128, (fsb_b, B_FSB), False, [B_KnT, B_Vn]))
                                attend(8, qsl, tiles, None, g, [True], fsl=slice(0, 8))
                                combine(8, g, 2, False, gsrc=gs_b)
                            for g in range(2):
                                cmp_part2(8, qsl, 128, g, gsrc=gs_b)
                            for g in range(2):
                                tiles = []
                                for kt in range(16):
                                    Fb = (F1, B_F1) if kt == 15 else None
                                    tiles.append((KE[g][0][:, kt * 128:(kt + 1) * 128], Vsel[:, kt, g, :], 128, Fb, True, [KE[g][1], B_Vsel]))
                                tiles.append((KE[g][0][:, 2048:2176], Vn[:, 0, g, :], 128, (fsb_b, B_FSB), True, [KE[g][1], B_Vn]))
                                attend(8, qsl, tiles, None, g, [True], fsl=slice(0, 8))
                                combine(8, g, 1, False, gsrc=gs_b)
                            kb.dma("sp", o_att_s[b * 8:(b + 1) * 8, :], o_att[0:8, :], r=[B_oatt], w=[B_oatts])
                    else:
                        kb.op("pool", lambda e: e.memset(o_att_s[:], 0.0), w=[B_oatts])
                    prev_pool = [(sp_bf[:, 0, :], 4, B_spbf), (sp_bf[:, 1, :], 5, B_spbf)]
                    finish_tile(128, xtt, B_xt, par, u_bf[par], prev_pool, (3,), xscr_s[:, :], B_xscr_s, oa=(o_att_s, B_oatts))
                kb.barrier()

            if do_ffn:
                with contextlib.ExitStack() as stC:
                    def sc(name, shape, dt):
                        return stC.enter_context(nc.sbuf_tensor(un(name), list(shape), dt)), Buf(name)
                    wo_sb, B_wo = sc("wo_sb", [128, 22, D], BF16)
                    gffn, B_gffn = sc("gffn", [128, D], F32)
                    gfin, B_gfin = sc("gfin", [128, D], F32)
                    wg = [sc("wg%d" % i, [128, 8, 512], BF16) for i in range(2)]
                    wu = [sc("wu%d" % i, [128, 8, 512], BF16) for i in range(2)]
                    xc = [sc("xc%d" % i, [128, D], F32) for i in range(4)]
                    hTc, B_hTc = sc("hTc", [128, 8, 512], BF16)
                    actT, B_actT = sc("actT", [128, 22, 512], BF16)
                    junk2, B_junk2 = sc("junk2", [128, D], BF16)
                    ss2, B_ss2 = sc("ss2", [128, 2], F32)
                    h2, B_h2 = sc("h2", [128, D], BF16)
                    sg = [sc("sg%d" % i, [128, 512], F32) for i in range(2)]
                    xo = [sc("xo%d" % i, [128, D], F32) for i in range(2)]
                    yo = [sc("yo%d" % i, [128, D], F32) for i in range(2)]
                    for j in range(22):
                        kb.dma("pool", wo_sb[:, j, :], w_ffn_out[l, j * 128:(j + 1) * 128, :], w=[B_wo])
                    kb.dma("sp", gffn[:], bass.AP(tensor=norm_ffn, offset=l * D, ap=[[0, 128], [1, D]]), w=[B_gffn])
                    kb.dma("sp", gfin[:], bass.AP(tensor=norm_final, offset=0, ap=[[0, 128], [1, D]]), w=[B_gfin])
                    chunks = [("p", c * 4, min(4, NT - c * 4)) for c in range((NT + 3) // 4)]
                    if do_sample:
                        chunks.append(("s", 0, 1))
                    wrot = [0]
                    orot = [0]
                    for (kind, t0, ntile) in chunks:
                        ntok = ntile * 128
                        for i in range(ntile):
                            xct, B_xc = xc[i]
                            if kind == "p":
                                kb.dma("sp", xct[:], xscr_p[(t0 + i) * 128:(t0 + i + 1) * 128, :], r=[B_xscr_p[t0 + i]], w=[B_xc])
                            else:
                                kb.dma("sp", xct[:], xscr_s[:, :], r=[B_xscr_s], w=[B_xc])
                            kb.op("act", lambda e, xct=xct: e.activation(out=junk2[:], in_=xct[:], func=AF.Square, accum_out=ss2[:, 0:1]), r=[B_xc], w=[B_junk2, B_ss2])
                            kb.op("dve", lambda e: e.tensor_scalar(out=ss2[:, 1:2], in0=ss2[:, 0:1], scalar1=1.0 / D, scalar2=EPS, op0=ALU.mult, op1=ALU.add), r=[B_ss2], w=[B_ss2])
                            kb.op("act", lambda e: e.activation(out=ss2[:, 1:2], in_=ss2[:, 1:2], func=AF.Sqrt), r=[B_ss2], w=[B_ss2])
                            kb.op("dve", lambda e: e.reciprocal(out=ss2[:, 1:2], in_=ss2[:, 1:2]), r=[B_ss2], w=[B_ss2])
                            kb.op("dve", lambda e, xct=xct: e.scalar_tensor_tensor(out=h2[:], in0=xct[:], scalar=ss2[:, 1:2], in1=gffn[:], op0=ALU.mult, op1=ALU.mult), r=[B_xc, B_ss2, B_gffn], w=[B_h2])
                            tbt, tbb = next_tb()
                            for k in range(8):
                                kb.op("pe", lambda e, tbt=tbt, k=k: e.transpose(out=tbt[:, k * 128:(k + 1) * 128], in_=h2[:, k * 128:(k + 1) * 128], identity=ident[:]), r=[B_h2, B_ident], w=[tbb])
                            kb.op("act", lambda e, tbt=tbt, i=i: e.copy(out=hTc[:, :, i * 128:(i + 1) * 128], in_=tbt[:, :].rearrange("p (a b) -> p a b", a=8)), r=[tbb], w=[B_hTc])
                        for jg in range(6):
                            nj = 4 if jg < 5 else 2
                            wrot[0] += 1
                            wgt, B_wg = wg[wrot[0] % 2]
                            wut, B_wu = wu[wrot[0] % 2]
                            for k in range(8):
                                kb.dma("pool", wgt[:, k, 0:nj * 128], w_ffn_in[l, k * 128:(k + 1) * 128, jg * 512:jg * 512 + nj * 128], w=[B_wg])
                                kb.dma("pool", wut[:, k, 0:nj * 128], w_ffn_in[l, k * 128:(k + 1) * 128, DFF + jg * 512:DFF + jg * 512 + nj * 128], w=[B_wu])
                            for jj in range(nj):
                                j = jg * 4 + jj
                                gt_, gb_ = next_wb()
                                for k in range(8):
                                    kb.op("pe", lambda e, gt_=gt_, wgt=wgt, jj=jj, k=k: e.matmul(gt_[:, 0:ntok], lhsT=wgt[:, k, jj * 128:(jj + 1) * 128], rhs=hTc[:, k, 0:ntok], start=(k == 0), stop=(k == 7)), r=[B_wg, B_hTc], w=[gb_])
                                ut_, ub_ = next_wb()
                                for k in range(8):
                                    kb.op("pe", lambda e, ut_=ut_, wut=wut, jj=jj, k=k: e.matmul(ut_[:, 0:ntok], lhsT=wut[:, k, jj * 128:(jj + 1) * 128], rhs=hTc[:, k, 0:ntok], start=(k == 0), stop=(k == 7)), r=[B_wu, B_hTc], w=[ub_])
                                sgt, sgb = sg[j % 2]
                                kb.op("act", lambda e, gt_=gt_, sgt=sgt: e.activation(out=sgt[:, 0:ntok], in_=gt_[:, 0:ntok], func=AF.Silu), r=[gb_], w=[sgb])
                                kb.op("dve", lambda e, ut_=ut_, sgt=sgt, j=j: e.tensor_tensor(out=actT[:, j, 0:ntok], in0=ut_[:, 0:ntok], in1=sgt[:, 0:ntok], op=ALU.mult), r=[ub_, sgb], w=[B_actT])
                        for i in range(ntile):
                            xct, B_xc = xc[i]
                            orot[0] += 1
                            xot, B_xo = xo[orot[0] % 2]
                            for hf in range(2):
                                wo_, wob_ = next_wb()
                                for j in range(22):
                                    kb.op("pe", lambda e, wo_=wo_, j=j, i=i, hf=hf: e.matmul(wo_[:, :], lhsT=actT[:, j, i * 128:(i + 1) * 128], rhs=wo_sb[:, j, hf * 512:(hf + 1) * 512], start=(j == 0), stop=(j == 21)), r=[B_actT, B_wo], w=[wob_])
                                kb.op("dve", lambda e, wo_=wo_, xot=xot, xct=xct, hf=hf: e.tensor_tensor(out=xot[:, hf * 512:(hf + 1) * 512], in0=wo_[:, :], in1=xct[:, hf * 512:(hf + 1) * 512], op=ALU.add), r=[wob_, B_xc], w=[B_xo])
                            if not last:
                                if kind == "p":
                                    kb.dma("sp", xscr_p[(t0 + i) * 128:(t0 + i + 1) * 128, :], xot[:], r=[B_xo], w=[B_xscr_p[t0 + i]])
                                else:
                                    kb.dma("sp", xscr_s[:, :], xot[:], r=[B_xo], w=[B_xscr_s])
                            else:
                                yot, B_yo = yo[orot[0] % 2]
                                kb.op("act", lambda e, xot=xot: e.activation(out=junk2[:], in_=xot[:], func=AF.Square, accum_out=ss2[:, 0:1]), r=[B_xo], w=[B_junk2, B_ss2])
                                kb.op("dve", lambda e: e.tensor_scalar(out=ss2[:, 1:2], in0=ss2[:, 0:1], scalar1=1.0 / D, scalar2=EPS, op0=ALU.mult, op1=ALU.add), r=[B_ss2], w=[B_ss2])
                                kb.op("act", lambda e: e.activation(out=ss2[:, 1:2], in_=ss2[:, 1:2], func=AF.Sqrt), r=[B_ss2], w=[B_ss2])
                                kb.op("dve", lambda e: e.reciprocal(out=ss2[:, 1:2], in_=ss2[:, 1:2]), r=[B_ss2], w=[B_ss2])
                                kb.op("dve", lambda e, xot=xot, yot=yot: e.scalar_tensor_tensor(out=yot[:], in0=xot[:], scalar=ss2[:, 1:2], in1=gfin[:], op0=ALU.mult, op1=ALU.mult), r=[B_xo, B_ss2, B_gfin], w=[B_yo])
                                if kind == "p":
                                    kb.dma("sp", y_p[(t0 + i) * 128:(t0 + i + 1) * 128, :], yot[:], r=[B_yo], final=True)
                                else:
                                    kb.dma("sp", y_s[:, :], yot[:], r=[B_yo], final=True)
                    kb.barrier()
        kb.emit()
    return nc


_LAST_NC = [None]


def build_safe(*a, **k):
    try:
        return build(*a, **k)
    except StopIteration:
        return _LAST_NC[0]


_CONST_KEYS = ("ident", "j32", "oh", "negrow", "ex", "fc", "pb", "selb", "neg64", "iota_p")


def make_in_map(inp, consts, L, pb_idx, sb0):
    f = lambda a: np.ascontiguousarray(np.asarray(a))
    nphys = inp["cache_cmp"].shape[1]
    m = {
        "xp": f(inp["x_prompt"][pb_idx]),
        "xs": f(np.asarray(inp["x_sample"])[sb0:sb0 + 16].reshape(128, D)),
        "ccmp": f(np.asarray(inp["cache_cmp"]).reshape(L, nphys * 128, 256)),
        "csel": f(np.asarray(inp["cache_sel"]).reshape(L, nphys * 128, 256)),
        "swin": f(np.asarray(inp["state_win"])[:, sb0:sb0 + 16].reshape(L, 16, 512, 256)),
        "spool": f(np.asarray(inp["state_pool"])[:, sb0:sb0 + 16]),
        "ptab": f(np.asarray(inp["page_table"])[sb0:sb0 + 16].astype(np.int32)),
        "rel_bias": f(inp["rel_bias"]),
        "norm_mix": f(inp["norm_mix"]),
        "norm_ffn": f(inp["norm_ffn"]),
        "norm_final": f(np.asarray(inp["norm_final"]).reshape(1, D)),
        "w_in": f(inp["w_in"]),
        "w_out": f(inp["w_out"]),
        "cmp_pos": f(inp["cmp_pos"]),
        "w_cmp": f(inp["w_cmp"]),
        "w_pool": f(inp["w_pool"]),
        "pool_scale": f(inp["pool_scale"]),
        "w_ffn_in": f(inp["w_ffn_in"]),
        "w_ffn_out": f(inp["w_ffn_out"]),
    }
    for k in _CONST_KEYS:
        m[k] = consts[k]
    return m


def kernel(**inputs):
    B, T, _ = inputs["x_prompt"].shape
    L = inputs["w_in"].shape[0]
    nphys = inputs["cache_cmp"].shape[1]
    nc = build(T, L, 16, nphys, do_sample=DO_SAMPLE)
    consts = host_consts(max(T, 2304))
    in_maps = [make_in_map(inputs, consts, L, c // 2, 16 * c) for c in range(NCORES)]
    res = run_bass_kernel_spmd(nc, in_maps, core_ids=list(range(NCORES)))
    rs = res.results
    y_p = np.stack([rs[2 * b]["y_p"] for b in range(B)])
    y_s = np.concatenate([rs[c]["y_s"].reshape(16, 8, D) for c in range(NCORES)], axis=0)

    def pk(name, shape_tail):
        return np.stack([rs[2 * b][name] for b in range(B)], axis=1).reshape((L, B) + shape_tail)

    def sk(name, per, shape_tail):
        return np.concatenate([rs[c][name].reshape((L, 16) + per) for c in range(NCORES)], axis=1).reshape((L, 128) + shape_tail)
    return (y_p.astype(np.float32), y_s.astype(np.float32),
            pk("ncmp_p", (T, 2, 2, 64)), pk("nsel_p", (T, 2, 2, 64)), pk("nwin_p", (512, 2, 2, 64)), pk("npool_p", (15, 512)),
            sk("ncmp_s", (8, 256), (8, 2, 2, 64)), sk("nsel_s", (8, 256), (8, 2, 2, 64)),
            sk("nwin_s", (512, 256), (512, 2, 2, 64)), sk("npool_s", (15, 512), (15, 512)))
```
